# Optimizing a Trainium2 kernel written in Bass

```python
import jax, jax.numpy as jnp
from jax import lax
import numpy as np

D_MODEL = 1024
BATCH = 8
SEQ = 4096
DEPTH = 1

D_MIX = D_MODEL
D_A = D_MIX // 2
D_B = D_MIX - D_A
HEADS_A = 4
HEAD_DIM_A = D_A // HEADS_A
HEADS_B = 8
CHUNK = 128
CONV_B_WIDTH = 31
D_FF = 2816
CONV_F_WIDTH = 3
D_IN = 2 * D_A + 2 * D_B
LN_EPS = 1e-5
DEEPNORM_ALPHA = (2.0 * DEPTH) ** 0.25
DEEPNORM_BETA = (8.0 * DEPTH) ** -0.25

kernel_name = "hybrid_sgu_conformer_convffn_deepnorm"


def _layernorm(x, g, b):
    xf = x.astype(jnp.float32)
    mu = jnp.mean(xf, axis=-1, keepdims=True)
    var = jnp.mean(jnp.square(xf - mu), axis=-1, keepdims=True)
    y = (xf - mu) * lax.rsqrt(var + LN_EPS)
    return (y * g.astype(jnp.float32) + b.astype(jnp.float32)).astype(x.dtype)


def _causal_depthwise_conv(x, w, b):
    k = w.shape[0]
    y = lax.conv_general_dilated(
        x, w[:, None, :].astype(x.dtype), window_strides=(1,),
        padding=[(k - 1, 0)], dimension_numbers=("NWC", "WIO", "NWC"),
        feature_group_count=x.shape[-1])
    return y + b


def setup_inputs(seed: int = 0) -> dict:
    key = jax.random.key(seed)
    ks = jax.random.split(key, 24)
    f32 = jnp.float32
    nrm = lambda k, shape, s: jax.random.normal(k, shape, f32) * s
    gain = lambda k, shape: 1.0 + nrm(k, shape, 0.02)
    return {
        "x": jax.random.normal(ks[0], (BATCH, SEQ, D_MODEL), f32),
        "w_in": nrm(ks[1], (D_MODEL, D_IN), D_MODEL ** -0.5),
        "b_in": nrm(ks[2], (D_IN,), 0.02),
        "ln_a_g": gain(ks[3], (HEADS_A, HEAD_DIM_A)),
        "ln_a_b": nrm(ks[4], (HEADS_A, HEAD_DIM_A), 0.02),
        "w_spatial": nrm(ks[5], (HEADS_A, CHUNK, CHUNK), CHUNK ** -0.5),
        "b_spatial": gain(ks[6], (HEADS_A, CHUNK)),
        "conv_b_w": nrm(ks[7], (CONV_B_WIDTH, D_B), CONV_B_WIDTH ** -0.5),
        "conv_b_b": nrm(ks[8], (D_B,), 0.02),
        "ln_b_g": gain(ks[9], (D_B,)),
        "ln_b_b": nrm(ks[10], (D_B,), 0.02),
        "w_out": nrm(ks[11], (D_MIX, D_MODEL), D_MIX ** -0.5 * DEEPNORM_BETA),
        "b_out": nrm(ks[12], (D_MODEL,), 0.02),
        "ln1_g": gain(ks[13], (D_MODEL,)),
        "ln1_b": nrm(ks[14], (D_MODEL,), 0.02),
        "w_up": nrm(ks[15], (D_MODEL, 2 * D_FF), D_MODEL ** -0.5),
        "conv_f_w": nrm(ks[16], (CONV_F_WIDTH, 2 * D_FF), CONV_F_WIDTH ** -0.5),
        "conv_f_b": nrm(ks[17], (2 * D_FF,), 0.02),
        "w_down": nrm(ks[18], (D_FF, D_MODEL), D_FF ** -0.5 * DEEPNORM_BETA),
        "ln2_g": gain(ks[19], (D_MODEL,)),
        "ln2_b": nrm(ks[20], (D_MODEL,), 0.02),
    }


def _token_mixing(x, w_in, b_in, ln_a_g, ln_a_b, w_spatial, b_spatial,
                  conv_b_w, conv_b_b, ln_b_g, ln_b_b, w_out, b_out):
    bsz, seq, _ = x.shape
    n_chunks = seq // CHUNK
    h = x @ w_in + b_in
    za, zb = h[..., :2 * D_A], h[..., 2 * D_A:]

    za = jax.nn.gelu(za, approximate=False)
    u, v = za[..., :D_A], za[..., D_A:]
    v = _layernorm(v.reshape(bsz, seq, HEADS_A, HEAD_DIM_A), ln_a_g, ln_a_b)
    v = v.reshape(bsz, n_chunks, CHUNK, HEADS_A, HEAD_DIM_A)
    causal = jnp.tril(jnp.ones((CHUNK, CHUNK), dtype=w_spatial.dtype))
    ws = w_spatial * causal
    sv = jnp.einsum("hts,bcshd->bcthd", ws, v) + b_spatial.T[None, None, :, :, None]
    y_a = u * sv.reshape(bsz, seq, D_A)

    a_b, g_b = zb[..., :D_B], zb[..., D_B:]
    yb = a_b * jax.nn.sigmoid(g_b)
    yb = _causal_depthwise_conv(yb, conv_b_w, conv_b_b)
    y_b = jax.nn.silu(_layernorm(yb, ln_b_g, ln_b_b))

    y = jnp.concatenate([y_a, y_b], axis=-1)
    return y @ w_out + b_out


def _conv_ffn(x, w_up, conv_f_w, conv_f_b, w_down):
    h = _causal_depthwise_conv(x @ w_up, conv_f_w, conv_f_b)
    gate, val = h[..., :D_FF], h[..., D_FF:]
    return (jax.nn.silu(gate) * val) @ w_down


def reference(x, w_in, b_in, ln_a_g, ln_a_b, w_spatial, b_spatial, conv_b_w, conv_b_b,
              ln_b_g, ln_b_b, w_out, b_out, ln1_g, ln1_b, w_up, conv_f_w, conv_f_b,
              w_down, ln2_g, ln2_b):
    for _ in range(DEPTH):
        mix = _token_mixing(x, w_in, b_in, ln_a_g, ln_a_b, w_spatial, b_spatial,
                            conv_b_w, conv_b_b, ln_b_g, ln_b_b, w_out, b_out)
        x = _layernorm(DEEPNORM_ALPHA * x + mix, ln1_g, ln1_b)
        ffn = _conv_ffn(x, w_up, conv_f_w, conv_f_b, w_down)
        x = _layernorm(DEEPNORM_ALPHA * x + ffn, ln2_g, ln2_b)
    return x
```

```python
import numpy as np
from contextlib import ExitStack
import concourse.bass as bass
import concourse.mybir as mybir
from concourse.bass_utils import run_bass_kernel_spmd

F32 = mybir.dt.float32
BF16 = mybir.dt.bfloat16
AF = mybir.ActivationFunctionType
ALU = mybir.AluOpType

SEQ = 4096
D = 1024
NT = 512
DFF = 2816
NPAIR = 22
EPS = 1e-5
ALPHA = 2.0 ** 0.25
ENGS = ("pe", "act", "dve", "pool", "sp")
N_DMA_SEMS = 28
LAST_LABELS = None


class Sched:
    def __init__(self, nc):
        self.nc = nc
        self.q = {e: [] for e in ENGS}
        self.cnt = {e: 0 for e in ENGS}
        self.dcnt = [0] * N_DMA_SEMS
        self.waited = {e: {} for e in ENGS}
        self.res = {}
        self.ctx = ""
        self.labels = {e: [] for e in ENGS}

    def task(self, eng, fn, reads=(), writes=(), dma=None, extra=()):
        deps = {}

        def add(tok):
            if tok is None:
                return
            k, v = tok
            if deps.get(k, 0) < v:
                deps[k] = v

        for r in reads:
            st = self.res.get(r)
            if st is not None:
                add(st[0])
        for w in writes:
            st = self.res.get(w)
            if st is not None:
                add(st[0])
                for t in st[1]:
                    add(t)
        for t in extra:
            add(t)
        if dma is None:
            self.cnt[eng] += 1
            tok = (eng, self.cnt[eng])
        else:
            self.dcnt[dma] += 16
            tok = (("d", dma), self.dcnt[dma])
        waits = []
        for k, v in deps.items():
            if k == eng and eng in ("pe", "sp"):
                continue
            if self.waited[eng].get(k, 0) >= v:
                continue
            self.waited[eng][k] = v
            waits.append((k, v))
        self.q[eng].append((waits, fn, tok))
        self.labels[eng].append((self.ctx, tok[0] if isinstance(tok[0], str) else "dma%d" % tok[0][1], tok[1], list(writes)[:2]))
        for r in reads:
            st = self.res.setdefault(r, [None, []])
            st[1].append(tok)
        for w in writes:
            self.res[w] = [tok, []]
        return tok

    def emit(self, final_waits=()):
        nc = self.nc
        with ExitStack() as es:
            sems = {}
            for e in ENGS:
                sems[e] = es.enter_context(nc.semaphore("c_" + e))
            for i in range(N_DMA_SEMS):
                sems[("d", i)] = es.enter_context(nc.semaphore("d_%d" % i))
            block = es.enter_context(nc.Block())

            def run(engname, eng):
                for waits, fn, tok in self.q[engname]:
                    for k, v in waits[1:]:
                        eng.wait_ge(sems[k], v)
                    ins = fn(eng)
                    if isinstance(ins, (list, tuple)):
                        first, last = ins[0], ins[-1]
                    else:
                        first = last = ins
                    if waits:
                        first._wait_ge(sems[waits[0][0]], waits[0][1])
                    k, v = tok
                    last.then_inc(sems[k], 1 if k == engname else 16)
                if engname == "sp":
                    for k, v in final_waits:
                        eng.wait_ge(sems[k], v)

            @block.tensor
            def _(eng):
                run("pe", eng)

            @block.scalar
            def _(eng):
                run("act", eng)

            @block.vector
            def _(eng):
                run("dve", eng)

            @block.gpsimd
            def _(eng):
                run("pool", eng)

            @block.sync
            def _(eng):
                run("sp", eng)


def build_nc(ntiles=SEQ // NT):
    nc = bass.Bass("TRN2", target_bir_lowering=False)
    seq = ntiles * NT

    def din(name, shape):
        return nc.dram_tensor(name, list(shape), F32, kind="ExternalInput").ap()

    x = din("x", [seq, D])
    w_in = din("w_in", [D, 2048])
    b_in = din("b_in", [2048])
    ln_a_g = din("ln_a_g", [4, 128])
    ln_a_b = din("ln_a_b", [4, 128])
    w_spatial = din("w_spatial", [4, 128, 128])
    b_spatial = din("b_spatial", [4, 128])
    conv_b_w = din("conv_b_w", [31, 512])
    conv_b_b = din("conv_b_b", [512])
    ln_b_g = din("ln_b_g", [512])
    ln_b_b = din("ln_b_b", [512])
    w_out = din("w_out", [D, D])
    b_out = din("b_out", [D])
    ln1_g = din("ln1_g", [D])
    ln1_b = din("ln1_b", [D])
    w_up = din("w_up", [D, 2 * DFF])
    conv_f_w = din("conv_f_w", [3, 2 * DFF])
    conv_f_b = din("conv_f_b", [2 * DFF])
    w_down = din("w_down", [DFF, D])
    ln2_g = din("ln2_g", [D])
    ln2_b = din("ln2_b", [D])
    out = nc.dram_tensor("out", [seq, D], F32, kind="ExternalOutput").ap()

    w_in_s = nc.dram_tensor("w_in_s", [8, 128, 8, 256], BF16, kind="Internal").ap()
    w_out_s = nc.dram_tensor("w_out_s", [4, 128, 4, 512], BF16, kind="Internal").ap()
    w_up_s = nc.dram_tensor("w_up_s", [NPAIR, 128, 8, 256], BF16, kind="Internal").ap()
    w_dn_s = nc.dram_tensor("w_dn_s", [12, 128, 4, 512], BF16, kind="Internal").ap()

    S = Sched(nc)
    T = S.task
    D_PAR, D_CIN, D_COUT, D_CUP, D_CDN = 0, 1, 2, 3, 4
    D_X = (5, 6)
    D_W = (7, 8, 9, 10)
    D_O = (11, 12)
    D_LR, D_LO = 13, 14
    D_XB = 15

    with ExitStack() as es:
        def sb(name, shape, dt):
            return es.enter_context(nc.sbuf_tensor(name, list(shape), dt))

        ident = sb("ident", [128, 128], BF16)
        identf = sb("identf", [128, 128], F32)
        ones_bf = sb("ones_bf", [128, 128], BF16)
        Dg32 = sb("Dg32", [128, 4, 31, 32], BF16)
        ident32 = sb("ident32", [128, 32], F32)
        wsT = sb("wsT", [128, 4, 128], BF16)
        Rh = sb("Rh", [128, 4, 128], F32)
        cols = sb("cols", [128, 40], F32)
        cwb = sb("cwb", [128, 4, 31], F32)
        cwf = sb("cwf", [128, 44, 3], F32)
        cbf = sb("cbf", [128, 44], F32)
        rows_hl = sb("rows_hl", [2, 1536], BF16)
        ones2 = sb("ones2", [2, 128], BF16)
        wsmb = sb("wsmb", [128, 4, 128], BF16)
        g1t = sb("g1t", [128, D], F32)
        b1t = sb("b1t", [128, D], F32)
        g2t = sb("g2t", [128, D], F32)
        b2t = sb("b2t", [128, D], F32)

        wsl = [sb("wsl%d" % i, [128, 2048], BF16) for i in range(4)]
        xs = [sb("xs%d" % i, [128, 4, D], F32) for i in range(2)]
        xb = [sb("xb%d" % i, [128, D], BF16) for i in range(2)]
        xbx = sb("xbx", [128, 4, D], BF16)
        xT = sb("xT", [128, 8, NT], BF16)
        x1T = sb("x1T", [128, 8, NT], BF16)
        uT = sb("uT", [128, 4, NT], BF16)
        vy = sb("vy", [128, 4, NT], F32)
        vn = sb("vn", [128, 4, NT], BF16)
        sg = [sb("sg%d" % i, [128, NT], F32) for i in range(4)]
        yb = sb("yb", [128, 4, NT + 32], BF16)
        ycb = [sb("ycb%d" % i, [128, NT], BF16) for i in range(2)]
        ycsq = [sb("ycsq%d" % i, [128, NT], BF16) for i in range(2)]
        st_mean = sb("st_mean", [128, NT], F32)
        st_tmp = sb("st_tmp", [128, NT], F32)
        yT = sb("yT", [128, 8, NT], BF16)
        hs = [sb("hs%d" % i, [128, 2, NT + 2], F32) for i in range(3)]
        gt = [sb("gt%d" % i, [128, 2, NT], F32) for i in range(3)]
        sgm = [sb("sgm%d" % i, [128, NT], F32) for i in range(2)]
        hsave = sb("hsave", [128, NPAIR, 2, 2], F32)
        actT = sb("actT", [128, NPAIR, NT], BF16)
        sm = sb("sm", [128, 112], F32)

        psum = [es.enter_context(nc.psum_tensor("ps%d" % i, [128, 512], F32)) for i in range(8)]
        xs1f = xs[1][:].rearrange("p b d -> p (b d)")
        rowsf = xs1f[0:2, 0:1536]
        rows_t = xs1f[0:2, 1536:3072]
        Lr = xs1f[0:2, 3072:3584].rearrange("p (h d) -> p h d", h=4)
        Rr = xs1f[0:2, 3584:4096].rearrange("p (h d) -> p h d", h=4)
        rows_lo = actT[:].rearrange("p j t -> p (j t)")[0:1, 0:1536]
        wsm = gt[0][:].rearrange("p a (h t) -> p (a h) t", h=4)[:, 0:4, :]
        XS1 = ["xs1_%d" % b for b in range(4)]
        st_rstd = st_tmp
        PA = [0, 1, 2, 3]
        PB = [4, 5, 6, 7]
        rrB = [0]

        def nextB():
            k = PB[rrB[0] % 4]
            rrB[0] += 1
            return k

        def pr(k):
            return "ps%d" % k

        DN_GROUPS = []
        for h in range(2):
            for g6 in range(6):
                c0 = g6 * 4
                DN_GROUPS.append((h, c0, min(4, NPAIR - c0)))
        S_CIN = (D_CIN, 22, 23)
        S_CUP = (D_CUP, 16, 17, 18)
        S_CDN = (D_CDN, 19, 20, 21)

        S_COUT = (D_COUT, 24, 25, 26)
        last_on = {}

        def cast_dma(sem, out_ap, in_ap, first_of_group=True):
            extra = [last_on[sem]] if (first_of_group and sem in last_on) else []
            tk = T("pool", lambda e: e.dma_start(out=out_ap, in_=in_ap), dma=sem, extra=extra)
            last_on[sem] = tk
            return tk

        def cast_in(g):
            tk = cast_dma(S_CIN[g % 3], w_in_s[g], w_in[:, g * 256:(g + 1) * 256].rearrange("(k p) c -> p k c", p=128))
            S.res["wsc_in%d" % g] = [tk, []]

        def cast_out(i):
            h, kh = i // 2, i % 2
            tk = cast_dma(S_COUT[i], w_out_s[i],
                          w_out[kh * 512:(kh + 1) * 512, h * 512:(h + 1) * 512].rearrange("(k p) c -> p k c", p=128))
            S.res["wsc_out%d" % i] = [tk, []]

        def cast_up(j):
            cast_dma(S_CUP[j % 4], w_up_s[j, :, :, 0:128],
                     w_up[:, j * 128:(j + 1) * 128].rearrange("(k p) c -> p k c", p=128))
            tk = cast_dma(S_CUP[j % 4], w_up_s[j, :, :, 128:256],
                          w_up[:, DFF + j * 128:DFF + (j + 1) * 128].rearrange("(k p) c -> p k c", p=128),
                          first_of_group=False)
            S.res["wsc_up%d" % j] = [tk, []]

        def cast_dn(gi):
            h, c0, ncnk = DN_GROUPS[gi]
            tk = cast_dma(S_CDN[gi % 4], w_dn_s[gi, :, 0:ncnk, :],
                          w_down[c0 * 128:(c0 + ncnk) * 128, h * 512:(h + 1) * 512].rearrange("(c p) m -> p c m", p=128))
            S.res["wsc_dn%d" % gi] = [tk, []]

        T("pool", lambda e: e.dma_start(out=xbx[:], in_=x[0:NT, :].rearrange("(b p) d -> p b d", p=128)),
          writes=["xbx%d" % b for b in range(4)], dma=D_XB)
        T("sp", lambda e: e.dma_start(out=xs[0][:], in_=x[0:NT, :].rearrange("(b p) d -> p b d", p=128)),
          writes=["xs0_%d" % b for b in range(4)], dma=D_X[0])

        def pload(dst, src):
            return T("sp", lambda e: e.dma_start(out=dst, in_=src, allow_slow_non_contiguous=True), dma=D_PAR)

        stg = hs[1][:].rearrange("p a t -> p (a t)")
        stg2 = hs[2][:].rearrange("p a t -> p (a t)")
        rowsrc = [b_in[0:512], b_in[1536:2048], b_in[1024:1536], None, conv_b_b, ln_b_g, ln_b_b]
        for i_, src_ in enumerate(rowsrc):
            if src_ is None:
                pload(stg[12:16, 0:128], ln_a_g)
            else:
                pload(stg[4 * i_:4 * i_ + 4, 0:128], src_.rearrange("(c p) -> c p", p=128))
        pload(stg[0:44, 128:256], conv_f_b.rearrange("(c p) -> c p", p=128))
        for k in range(3):
            pload(stg[0:44, 256 + 128 * k:384 + 128 * k], conv_f_w[k].rearrange("(c p) -> c p", p=128))
        pload(stg2[0:31, 0:512], conv_b_w)
        pload(rowsf[0:1, 0:512], b_in[512:1024].rearrange("(o n) -> o n", o=1))
        pload(rowsf[0:1, 512:1536], b_out.rearrange("(o n) -> o n", o=1))
        pload(Rr[1:2, :, :], b_spatial.rearrange("(o h) t -> o h t", o=1))
        t_par = pload(wsm[:, :, :], w_spatial.rearrange("h t s -> t h s"))
        T("pool", lambda e: e.memset(Lr[:], 1.0), writes=["Lr"])
        t_lr = T("sp", lambda e: e.dma_start(out=Lr[0:1, :, :], in_=ln_a_b.rearrange("(o h) d -> o h d", o=1)),
                 reads=["Lr"], dma=D_LR)
        t_big = None
        for dst_, src_ in ((g1t, ln1_g), (b1t, ln1_b), (g2t, ln2_g), (b2t, ln2_b)):
            t_big = T("sp", lambda e, dst_=dst_, src_=src_: e.dma_start(out=dst_[:, :], in_=src_.partition_broadcast(128)),
                      dma=27)
        S.res["Pbig"] = [t_big, []]
        S.res["P"] = [t_par, []]

        T("pool", lambda e: e.memset(identf[:], 0.0), writes=["identf"])
        T("pool", lambda e: e.affine_select(out=identf[:], in_=identf[:], pattern=[[-1, 128]],
                                            compare_op=ALU.not_equal, fill=1.0, base=0, channel_multiplier=1),
          reads=["identf"], writes=["identf"])
        T("pool", lambda e: e.tensor_copy(out=ident[:], in_=identf[:]), reads=["identf"], writes=["ident"])
        T("pool", lambda e: e.memset(ones_bf[:], 1.0), writes=["ones_bf"])
        T("pool", lambda e: e.memset(ones2[:], 1.0), writes=["ones2"])
        T("pool", lambda e: e.memset(hsave[:], 0.0), writes=["hsave"])
        T("pool", lambda e: e.memset(yb[:], 0.0), writes=["yb0", "yb1", "yb2", "yb3"])

        T("dve", lambda e: e.tensor_copy(out=rows_hl[0:1, :], in_=rowsf[0:1, :]), reads=["P"], writes=["rows_hi"])
        T("dve", lambda e: e.tensor_copy(out=rows_t[0:1, :], in_=rows_hl[0:1, :]), reads=["rows_hi"], writes=["rows_t"])
        T("dve", lambda e: e.tensor_tensor(out=rows_t[0:1, :], in0=rowsf[0:1, :], in1=rows_t[0:1, :], op=ALU.subtract),
          reads=["rows_t", "P"], writes=["rows_t"])
        T("dve", lambda e: e.tensor_copy(out=rows_lo[0:1, :], in_=rows_t[0:1, :]), reads=["rows_t"], writes=["rows_lo"])
        T("sp", lambda e: e.dma_start(out=rows_hl[1:2, :], in_=rows_lo[0:1, :]), reads=["rows_lo", "rows_hi"],
          writes=["rows_hl"], dma=D_LO)

        pkp = nextB()

        def par_transposes(e):
            o = []
            pp = psum[pkp]
            o.append(e.transpose(out=pp[:, 0:28], in_=stg[0:28, 0:128], identity=identf[0:28, 0:28]))
            o.append(e.transpose(out=pp[:, 32:76], in_=stg[0:44, 128:256], identity=identf[0:44, 0:44]))
            for k in range(3):
                o.append(e.transpose(out=pp[:, 80 + 48 * k:124 + 48 * k], in_=stg[0:44, 256 + 128 * k:384 + 128 * k],
                                     identity=identf[0:44, 0:44]))
            for c in range(4):
                o.append(e.transpose(out=pp[:, 224 + 32 * c:255 + 32 * c], in_=stg2[0:31, c * 128:(c + 1) * 128],
                                     identity=identf[0:31, 0:31]))
            return o
        T("pe", par_transposes, reads=["P", "identf"], writes=[pr(pkp)])
        T("dve", lambda e: e.tensor_copy(out=cols[:, 0:28], in_=psum[pkp][:, 0:28]), reads=[pr(pkp)], writes=["cols"])
        T("dve", lambda e: e.tensor_copy(out=cbf[:, :], in_=psum[pkp][:, 32:76]), reads=[pr(pkp)], writes=["cbf"])
        T("dve", lambda e: e.tensor_copy(out=cwf[:].rearrange("p c k -> p k c"),
                                         in_=psum[pkp][:, 80:224].rearrange("p (k c) -> p k c", k=3)[:, :, 0:44]),
          reads=[pr(pkp)], writes=["cwf"])
        T("dve", lambda e: e.tensor_copy(out=cwb[:],
                                         in_=psum[pkp][:, 224:352].rearrange("p (c k) -> p c k", c=4)[:, :, 0:31]),
          reads=[pr(pkp)], writes=["cwb"])
        S.res["P"] = [("dve", S.cnt["dve"]), []]
        S.res["Pdma"] = [t_par, []]

        for h in range(4):
            T("pool", lambda e, h=h: e.affine_select(out=wsm[:, h, :], in_=wsm[:, h, :], pattern=[[-1, 128]],
                                                     compare_op=ALU.is_ge, fill=0.0, base=0, channel_multiplier=1),
              reads=["P"], writes=["wsm%d" % h])
            T("pool", lambda e, h=h: e.tensor_copy(out=wsmb[:, h, :], in_=wsm[:, h, :]), reads=["wsm%d" % h],
              writes=["wsmb%d" % h])
        pk = nextB()
        T("pe", lambda e: [e.transpose(out=psum[pk][:].bitcast(BF16)[:, h * 128:(h + 1) * 128], in_=wsmb[:, h, :],
                                       identity=ident[:]) for h in range(4)],
          reads=["wsmb0", "wsmb1", "wsmb2", "wsmb3", "ident"], writes=[pr(pk)])
        T("dve", lambda e: e.tensor_copy(out=wsT[:].rearrange("p h t -> p (h t)"), in_=psum[pk][:].bitcast(BF16)[:, 0:512]),
          reads=[pr(pk)], writes=["wsT"])
        pk2 = nextB()
        T("pe", lambda e: [e.matmul(psum[pk2][0:1, h * 128:(h + 1) * 128], lhsT=ones_bf[:, 0:1], rhs=wsT[:, h, :],
                                    start=True, stop=True) for h in range(4)],
          reads=["wsT", "ones_bf"], writes=[pr(pk2)])
        T("dve", lambda e: e.tensor_copy(out=Rr[0:1, :, :].rearrange("p h t -> p (h t)"), in_=psum[pk2][0:1, :]),
          reads=[pr(pk2), "P"], writes=["Rr"])
        pk3 = nextB()
        T("pe", lambda e: [e.matmul(psum[pk3][:, h * 128:(h + 1) * 128], lhsT=Lr[:, h, :], rhs=Rr[:, h, :],
                                    start=True, stop=True) for h in range(4)],
          reads=["Rr", "Lr", "P"], writes=[pr(pk3)], extra=[t_lr])
        T("dve", lambda e: e.tensor_copy(out=Rh[:].rearrange("p h t -> p (h t)"), in_=psum[pk3][:]),
          reads=[pr(pk3)], writes=["Rh"])

        def build_dg32():
            T("dve", lambda e: e.tensor_tensor(out=ident32[:], in0=identf[:, 0:32], in1=identf[:, 32:64], op=ALU.add),
              reads=["identf"], writes=["ident32"])
            T("dve", lambda e: e.tensor_tensor(out=ident32[:], in0=ident32[:], in1=identf[:, 64:96], op=ALU.add),
              reads=["identf", "ident32"], writes=["ident32"])
            T("dve", lambda e: e.tensor_tensor(out=ident32[:], in0=ident32[:], in1=identf[:, 96:128], op=ALU.add),
              reads=["identf", "ident32"], writes=["ident32"])
            for c in range(4):
                for k in range(31):
                    T("dve", lambda e, c=c, k=k: e.tensor_scalar(out=Dg32[:, c, k, :], in0=ident32[:], scalar1=cwb[:, c, k:k + 1],
                                                                 scalar2=None, op0=ALU.mult),
                      reads=["ident32", "P"], writes=["Dg%d_%d" % (c, k)])


        prep_toks = [(e_, S.cnt[e_]) for e_ in ("pe", "act", "dve", "pool") if S.cnt[e_] > 0]
        prep_toks += [(("d", D_PAR), S.dcnt[D_PAR]), (("d", D_LR), S.dcnt[D_LR]), (("d", D_LO), S.dcnt[D_LO])]
        for r_ in XS1 + ["actT0", "actT1", "actT2", "gt0_0", "gt0_1", "hs1_0", "hs1_1", "hs2_0", "hs2_1", "hsh1", "hsh2"]:
            S.res.setdefault(r_, [None, []])[1].extend(prep_toks)

        wcount = [0]

        def wload(src_ap, ncols, scr_res):
            i = wcount[0]
            wcount[0] += 1
            s = i % 4
            T("sp", lambda e: e.dma_start(out=wsl[s][:, 0:ncols], in_=src_ap), reads=[scr_res], writes=["wsl%d" % s],
              dma=D_W[s])
            return s

        def xload_bf(t):
            T("pool", lambda e: e.dma_start(out=xbx[:], in_=x[t * NT:(t + 1) * NT, :].rearrange("(b p) d -> p b d", p=128)),
              writes=["xbx%d" % b for b in range(4)], dma=D_XB)

        def xload(t):
            s = t % 2
            T("sp", lambda e: e.dma_start(out=xs[s][:], in_=x[t * NT:(t + 1) * NT, :].rearrange("(b p) d -> p b d", p=128)),
              writes=["xs%d_%d" % (s, b) for b in range(4)], dma=D_X[s])

        INV_SQRT2 = 0.7071067811865476
        cols2 = sb("cols2", [128, 8], F32)
        T("dve", lambda e: e.tensor_scalar(out=cols2[:, 0:4], in0=cols[:, 0:4], scalar1=INV_SQRT2, scalar2=None,
                                           op0=ALU.mult), reads=["P"], writes=["cols2a"])
        T("dve", lambda e: e.tensor_scalar(out=cols2[:, 4:8], in0=cols[:, 0:4], scalar1=0.5, scalar2=None,
                                           op0=ALU.mult), reads=["P"], writes=["cols2b"])
        sm2 = sb("sm2", [128, 40], F32)
        epsc = sb("epsc", [128, 1], F32)
        T("pool", lambda e: e.memset(epsc[:], EPS), writes=["epsc"])
        rrS = [0]

        def nextS():
            i = rrS[0] % len(sg)
            rrS[0] += 1
            return i

        def transposes(src_tile, dstT, s, tagr, from_bf=False):
            for b in range(4):
                xbi = b % 2
                if from_bf:
                    srcb, sres = xbx[:, b, :], "xbx%d" % b
                else:
                    T("act", lambda e, b=b, xbi=xbi: e.activation(out=xb[xbi][:], in_=src_tile[:, b, :], func=AF.Copy),
                      reads=["xs%d_%d" % (s, b)], writes=["xb%d" % xbi])
                    srcb, sres = xb[xbi][:], "xb%d" % xbi
                k = nextB()
                T("pe", lambda e, k=k, srcb=srcb: [e.transpose(out=psum[k][:].bitcast(BF16)[:, c * 128:(c + 1) * 128],
                                                               in_=srcb[:, c * 128:(c + 1) * 128], identity=ident[:])
                                                   for c in range(8)],
                  reads=[sres, "ident"], writes=[pr(k)])
                T("dve", lambda e, k=k, b=b: e.tensor_copy(
                    out=dstT[:, :, b * 128:(b + 1) * 128],
                    in_=psum[k][:].bitcast(BF16).rearrange("p (c t) -> p c t", c=8)),
                  reads=[pr(k)], writes=["%s_%d" % (tagr, b)])
                if b % 2 == 1:
                    yield

        def gelu_from_psum(k, out_ap, out_res, bias_s=None, bias_h=None):
            ia, ib = nextS(), nextS()
            if bias_s is None:
                T("act", lambda e: e.activation(out=sg[ia][:], in_=psum[k][:], func=AF.Erf, scale=INV_SQRT2),
                  reads=[pr(k)], writes=["sg%d" % ia])
                T("act", lambda e: e.activation(out=sg[ib][:], in_=psum[k][:], func=AF.Identity, scale=0.5),
                  reads=[pr(k)], writes=["sg%d" % ib])
            else:
                T("act", lambda e: e.activation(out=sg[ia][:], in_=psum[k][:], func=AF.Erf, scale=INV_SQRT2, bias=bias_s),
                  reads=[pr(k), "cols2a"], writes=["sg%d" % ia])
                T("act", lambda e: e.activation(out=sg[ib][:], in_=psum[k][:], func=AF.Identity, scale=0.5, bias=bias_h),
                  reads=[pr(k), "cols2b"], writes=["sg%d" % ib])
            T("dve", lambda e: e.scalar_tensor_tensor(out=out_ap, in0=sg[ia][:], scalar=1.0, in1=sg[ib][:],
                                                      op0=ALU.add, op1=ALU.mult),
              reads=["sg%d" % ia, "sg%d" % ib], writes=[out_res])

        def mixer_a(t):
            s = t % 2
            xst = xs[s]
            xT_all = ["xT_%d" % b for b in range(4)]
            kM, kQ = PB[2], PB[3]

            def inproj_fm(wv, cc, ws_):
                k = nextB()
                T("pe", lambda e: [
                    e.matmul(psum[k][:], lhsT=wv[:, kk, cc * 128:(cc + 1) * 128], rhs=xT[:, kk, :],
                             start=(kk == 0), stop=(kk == 7)) for kk in range(8)],
                  reads=xT_all + ["wsl%d" % ws_], writes=[pr(k)])
                return k

            def stats_mm(c):
                T("pe", lambda e: e.matmul(psum[kM][:], lhsT=ones_bf[:], rhs=ycb[c % 2][:], start=(c == 0), stop=(c == 3)),
                  reads=["ycb%d" % (c % 2), "ones_bf"], writes=[pr(kM)])
                T("pe", lambda e: e.matmul(psum[kQ][:], lhsT=ones_bf[:], rhs=ycsq[c % 2][:], start=(c == 0), stop=(c == 3)),
                  reads=["ycsq%d" % (c % 2), "ones_bf"], writes=[pr(kQ)])

            def lnb_stats():
                T("dve", lambda e: e.tensor_scalar(out=st_mean[:], in0=psum[kM][:], scalar1=1.0 / 512, scalar2=None,
                                                   op0=ALU.mult), reads=[pr(kM)], writes=["st_mean"])
                T("dve", lambda e: e.tensor_tensor(out=st_tmp[:], in0=st_mean[:], in1=st_mean[:], op=ALU.mult),
                  reads=["st_mean"], writes=["st_tmp"])
                T("dve", lambda e: e.scalar_tensor_tensor(out=st_tmp[:], in0=psum[kQ][:], scalar=1.0 / 512, in1=st_tmp[:],
                                                          op0=ALU.mult, op1=ALU.subtract),
                  reads=[pr(kQ), "st_tmp"], writes=["st_tmp"])
                T("act", lambda e: e.activation(out=st_tmp[:], in_=st_tmp[:], func=AF.Ln, bias=epsc[:, 0:1], scale=1.0),
                  reads=["st_tmp", "epsc"], writes=["st_tmp"])
                T("act", lambda e: e.activation(out=st_tmp[:], in_=st_tmp[:], func=AF.Exp, scale=-0.5),
                  reads=["st_tmp"], writes=["st_tmp"])

            def lnb_chunk(c):
                si, sj = nextS(), nextS()
                T("dve", lambda e: e.tensor_tensor(out=sg[si][:], in0=vy[:, c, :], in1=st_mean[:], op=ALU.subtract),
                  reads=["vy%d" % c, "st_mean"], writes=["sg%d" % si])
                T("dve", lambda e: e.scalar_tensor_tensor(out=sg[si][:], in0=sg[si][:], scalar=cols[:, 20 + c:21 + c],
                                                          in1=st_tmp[:], op0=ALU.mult, op1=ALU.mult),
                  reads=["sg%d" % si, "st_tmp", "P"], writes=["sg%d" % si])
                T("act", lambda e: e.activation(out=sg[sj][:], in_=sg[si][:], func=AF.Sigmoid,
                                                bias=cols[:, 24 + c:25 + c], scale=1.0),
                  reads=["sg%d" % si, "P"], writes=["sg%d" % sj])
                T("dve", lambda e: e.scalar_tensor_tensor(out=yT[:, 4 + c, :], in0=sg[si][:], scalar=cols[:, 24 + c:25 + c],
                                                          in1=sg[sj][:], op0=ALU.add, op1=ALU.mult),
                  reads=["sg%d" % si, "sg%d" % sj, "P"], writes=["yT%d" % (4 + c)])

            yield from transposes(xst, xT, s, "xT", from_bf=True)
            yield
            if t > 0:
                T("pool", lambda e: e.tensor_copy(out=yb[:, :, 2:32], in_=yb[:, :, NT + 2:NT + 32]),
                  reads=["yb%d" % c for c in range(4)], writes=["yb%d" % c for c in range(4)])
            for g in range(2):
                ws_ = wload(w_in_s[6 + g].rearrange("p k c -> p (k c)"), 2048, "wsc_in%d" % (6 + g))
                wv = wsl[ws_][:].rearrange("p (k c) -> p k c", k=8)
                sidx = {}
                for cc in range(2):
                    c = g * 2 + cc
                    k = inproj_fm(wv, cc, ws_)
                    si = nextS()
                    sidx[c] = si
                    T("act", lambda e, k=k, c=c, si=si: e.activation(out=sg[si][:], in_=psum[k][:], func=AF.Sigmoid,
                                                                    bias=cols[:, 4 + c:5 + c], scale=1.0),
                      reads=[pr(k), "P"], writes=["sg%d" % si])
                yield
                ws2 = wload(w_in_s[4 + g].rearrange("p k c -> p (k c)"), 2048, "wsc_in%d" % (4 + g))
                wv2 = wsl[ws2][:].rearrange("p (k c) -> p k c", k=8)
                for cc in range(2):
                    c = g * 2 + cc
                    k = inproj_fm(wv2, cc, ws2)
                    si = sidx[c]
                    T("dve", lambda e, k=k, c=c, si=si: e.scalar_tensor_tensor(
                        out=yb[:, c, 32:32 + NT], in0=psum[k][:], scalar=cols[:, 8 + c:9 + c], in1=sg[si][:],
                        op0=ALU.add, op1=ALU.mult),
                      reads=[pr(k), "sg%d" % si, "P"], writes=["yb%d" % c])
                yield
            for c in range(4):
                k = PB[c % 2]
                T("pe", lambda e, k=k, c=c: [
                    e.matmul(psum[k][32 * q:32 * q + 32, :], lhsT=Dg32[32 * q:32 * q + 32, c, kk, :],
                             rhs=yb[32 * q:32 * q + 32, c, 2 + kk:2 + kk + NT],
                             start=(kk == 0), stop=(kk == 30), tile_position=(32 * q, 32 * q))
                    for kk in range(31) for q in range(4)],
                  reads=["yb%d" % c] + ["Dg%d_%d" % (c, kk) for kk in range(31)], writes=[pr(k)])
                T("act", lambda e, k=k, c=c: e.activation(out=ycb[c % 2][:], in_=psum[k][:], func=AF.Identity,
                                                          bias=cols[:, 16 + c:17 + c], scale=1.0),
                  reads=[pr(k), "P"], writes=["ycb%d" % (c % 2)])
                T("act", lambda e, k=k, c=c: e.activation(out=ycsq[c % 2][:], in_=psum[k][:], func=AF.Square,
                                                          bias=cols[:, 16 + c:17 + c], scale=1.0),
                  reads=[pr(k), "P"], writes=["ycsq%d" % (c % 2)])
                T("act", lambda e, k=k, c=c: e.activation(out=vy[:, c, :], in_=psum[k][:], func=AF.Identity,
                                                          bias=cols[:, 16 + c:17 + c], scale=1.0),
                  reads=[pr(k), "P"], writes=["vy%d" % c])
                if c > 0:
                    stats_mm(c - 1)
                yield
            for g in range(2):
                ws_ = wload(w_in_s[g].rearrange("p k c -> p (k c)"), 2048, "wsc_in%d" % g)
                wv = wsl[ws_][:].rearrange("p (k c) -> p k c", k=8)
                for cc in range(2):
                    c = g * 2 + cc
                    k = inproj_fm(wv, cc, ws_)
                    gelu_from_psum(k, uT[:, c, :], "uT%d" % c, cols2[:, c:c + 1], cols2[:, 4 + c:5 + c])
                    if c == 0:
                        stats_mm(3)
                        lnb_stats()
                if g == 1:
                    lnb_chunk(0)
                yield
            vbanks = [PB[2], PB[3], PB[0], PB[1]]
            for g in range(2):
                ws_ = wload(w_in_s[2 + g].rearrange("p k c -> p (k c)"), 2048, "wsc_in%d" % (2 + g))
                wv = wsl[ws_][:].rearrange("p (k c) -> p k c", k=8)
                for b in range(4):
                    k = vbanks[b]
                    T("pe", lambda e, k=k, wv=wv, b=b, g=g: [
                        e.matmul(psum[k][:, g * 256:(g + 1) * 256], lhsT=xT[:, kk, b * 128:(b + 1) * 128],
                                 rhs=wv[:, kk, :], start=(kk == 0), stop=False) for kk in range(8)] + [
                        e.matmul(psum[k][:, g * 256:(g + 1) * 256], lhsT=ones2[:, :],
                                 rhs=rows_hl[:, g * 256:(g + 1) * 256], start=False, stop=True)],
                      reads=["xT_%d" % b, "wsl%d" % ws_, "rows_hl", "ones2"], writes=[pr(k)])
                lnb_chunk(1 + g)
                yield

            def v_block(b):
                k = vbanks[b]
                gelu_from_psum(k, vy[:, b, :], "vy%d" % b)
                for h in range(4):
                    T("dve", lambda e, b=b, h=h: e.bn_stats(out=sm[:, h * 6:h * 6 + 6],
                                                            in_=vy[:, b, h * 128:(h + 1) * 128]),
                      reads=["vy%d" % b], writes=["smst%d" % h])
                    i16 = b * 4 + h
                    T("dve", lambda e, h=h, i16=i16: e.bn_aggr(out=sm[:, 32 + i16 * 2:34 + i16 * 2],
                                                               in_=sm[:, h * 6:h * 6 + 6]),
                      reads=["smst%d" % h], writes=["smmv%d" % i16])

            v_block(0)
            v_block(1)
            lnb_chunk(3)
            yield
            v_block(2)
            v_block(3)
            mvv = sm[:, 32:64].rearrange("p (i t) -> p i t", t=2)
            allmv = ["smmv%d" % i for i in range(16)]
            T("act", lambda e: e.activation(out=sm[:, 96:112], in_=mvv[:, :, 1], func=AF.Ln, bias=epsc[:, 0:1], scale=1.0),
              reads=allmv + ["epsc"], writes=["smln"])
            T("act", lambda e: e.activation(out=sm[:, 64:80], in_=sm[:, 96:112], func=AF.Exp, scale=-0.5),
              reads=["smln"], writes=["smrs"])
            T("dve", lambda e: e.scalar_tensor_tensor(out=sm[:, 80:96], in0=mvv[:, :, 0], scalar=-1.0, in1=sm[:, 64:80],
                                                      op0=ALU.mult, op1=ALU.mult),
              reads=allmv + ["smrs"], writes=["smnb"])
            for b in range(4):
                for h in range(4):
                    i16 = b * 4 + h
                    T("dve", lambda e, b=b, h=h, i16=i16: e.tensor_scalar(
                        out=vn[:, b, h * 128:(h + 1) * 128], in0=vy[:, b, h * 128:(h + 1) * 128],
                        scalar1=sm[:, 64 + i16:65 + i16], scalar2=sm[:, 80 + i16:81 + i16], op0=ALU.mult, op1=ALU.add),
                      reads=["vy%d" % b, "smrs", "smnb"], writes=["vn%d_%d" % (b, h)])
            yield
            yield
            yield
            for h in range(4):
                k = nextB()
                si = nextS()
                T("pe", lambda e, k=k, h=h: [
                    e.matmul(psum[k][:, b * 128:(b + 1) * 128], lhsT=vn[:, b, h * 128:(h + 1) * 128], rhs=wsT[:, h, :],
                             start=True, stop=True) for b in range(4)],
                  reads=["vn%d_%d" % (b, h) for b in range(4)] + ["wsT"], writes=[pr(k)])
                T("dve", lambda e, k=k, h=h, si=si: e.scalar_tensor_tensor(
                    out=sg[si][:].rearrange("p (b t) -> p b t", b=4),
                    in0=psum[k][:].rearrange("p (b t) -> p b t", b=4), scalar=cols[:, 12 + h:13 + h],
                    in1=Rh[:, h:h + 1, :].broadcast_to([128, 4, 128]), op0=ALU.mult, op1=ALU.add),
                  reads=[pr(k), "Rh", "P"], writes=["sg%d" % si])
                T("dve" if t == 0 else "pool",
                  lambda e, h=h, si=si: e.tensor_tensor(out=yT[:, h, :], in0=sg[si][:], in1=uT[:, h, :], op=ALU.mult),
                  reads=["sg%d" % si, "uT%d" % h], writes=["yT%d" % h])
            yield
            yield
            yield
            yT_all = ["yT%d" % c for c in range(8)]
            for h in range(2):
                for ki, kh in enumerate((1, 0)):
                    ws_ = wload(w_out_s[h * 2 + kh].rearrange("p k c -> p (k c)"), 2048, "wsc_out%d" % (h * 2 + kh))
                    wv = wsl[ws_][:].rearrange("p (k c) -> p k c", k=4)
                    for b in range(4):
                        k = PB[b]
                        T("pe", lambda e, k=k, wv=wv, b=b, kh=kh, h=h, ki=ki: [
                            e.matmul(psum[k][:], lhsT=yT[:, kh * 4 + kk, b * 128:(b + 1) * 128], rhs=wv[:, kk, :],
                                     start=(ki == 0 and kk == 0), stop=False) for kk in range(4)] + ([
                            e.matmul(psum[k][:], lhsT=ones2[:, :], rhs=rows_hl[:, 512 + h * 512:512 + (h + 1) * 512],
                                     start=False, stop=True)] if ki == 1 else []),
                          reads=["yT%d" % (kh * 4 + kk) for kk in range(4)] + ["wsl%d" % ws_, "rows_hl", "ones2"],
                          writes=[pr(k)])
                    if ki == 1:
                        for b in range(4):
                            k = PB[b]
                            T("dve", lambda e, k=k, b=b, h=h: e.scalar_tensor_tensor(
                                out=xst[:, b, h * 512:(h + 1) * 512], in0=xst[:, b, h * 512:(h + 1) * 512], scalar=ALPHA,
                                in1=psum[k][:], op0=ALU.mult, op1=ALU.add),
                              reads=[pr(k), "xs%d_%d" % (s, b)], writes=["xs%d_%d" % (s, b)])
                        if h == 0:
                            yield
                    yield
            yield from layernorm_gen(xst, s, g1t, b1t, ndve=(4 if t == 0 else 3))

        def mixer_b(t):
            s = t % 2
            src_tile = xs[s]

            def cast(b):
                T("act", lambda e: e.activation(out=xb[b % 2][:], in_=src_tile[:, b, :], func=AF.Copy),
                  reads=["xs%d_%d" % (s, b)], writes=["xb%d" % (b % 2)])
            cast(0)
            yield
            for b in range(4):
                if b + 1 < 4:
                    cast(b + 1)
                k = nextB()
                srcb = xb[b % 2][:]
                T("pe", lambda e, k=k, srcb=srcb: [e.transpose(out=psum[k][:].bitcast(BF16)[:, c * 128:(c + 1) * 128],
                                                               in_=srcb[:, c * 128:(c + 1) * 128], identity=ident[:])
                                                   for c in range(8)],
                  reads=["xb%d" % (b % 2), "ident"], writes=[pr(k)])
                T("dve", lambda e, k=k, b=b: e.tensor_copy(
                    out=x1T[:, :, b * 128:(b + 1) * 128],
                    in_=psum[k][:].bitcast(BF16).rearrange("p (c t) -> p c t", c=8)),
                  reads=[pr(k)], writes=["x1T_%d" % b])
                yield

        def layernorm_gen(xst, s, gt_, bt_, after=None, ndve=2):
            rns = ["xs%d_%d" % (s, b) for b in range(4)]
            for b in range(4):
                T("dve", lambda e, b=b: e.bn_stats(out=sm2[:, 0:6], in_=xst[:, b, 0:512]), reads=[rns[b]], writes=["sm2a"])
                T("dve", lambda e, b=b: e.bn_stats(out=sm2[:, 6:12], in_=xst[:, b, 512:1024]), reads=[rns[b]], writes=["sm2b"])
                T("dve", lambda e, b=b: e.bn_aggr(out=sm2[:, 16 + 2 * b:18 + 2 * b], in_=sm2[:, 0:12]),
                  reads=["sm2a", "sm2b"], writes=["sm2mv%d" % b])
            yield
            mv4 = sm2[:, 16:24].rearrange("p (b t) -> p b t", t=2)
            allmv = ["sm2mv%d" % b for b in range(4)]
            T("act", lambda e: e.activation(out=sm2[:, 24:28], in_=mv4[:, :, 1], func=AF.Ln, bias=epsc[:, 0:1], scale=1.0),
              reads=allmv + ["epsc"], writes=["sm2ln"])
            T("act", lambda e: e.activation(out=sm2[:, 28:32], in_=sm2[:, 24:28], func=AF.Exp, scale=-0.5),
              reads=["sm2ln"], writes=["sm2rs"])
            yield
            for b in range(4):
                T("dve", lambda e, b=b: e.scalar_tensor_tensor(out=xst[:, b, :], in0=xst[:, b, :],
                                                                scalar=sm2[:, 16 + 2 * b:17 + 2 * b], in1=gt_[:],
                                                                op0=ALU.subtract, op1=ALU.mult),
                  reads=[rns[b], "sm2mv%d" % b, "Pbig"], writes=[rns[b]])
                T("dve", lambda e, b=b: e.scalar_tensor_tensor(out=xst[:, b, :], in0=xst[:, b, :],
                                                                scalar=sm2[:, 28 + b:29 + b], in1=bt_[:],
                                                                op0=ALU.mult, op1=ALU.add),
                  reads=[rns[b], "sm2rs", "Pbig"], writes=[rns[b]])
                if after is not None:
                    after(b)
                if b % 2 == 1:
                    yield

        def layernorm_all(xst, s, gt_, bt_, after=None):
            for _ in layernorm_gen(xst, s, gt_, bt_, after):
                pass

        def ln2_store(t):
            s = t % 2
            xst = xs[s]
            yield from layernorm_gen(xst, s, g2t, b2t, after=lambda b: T(
                "pool", lambda e, b=b: e.dma_start(out=out[t * NT + b * 128:t * NT + (b + 1) * 128, :], in_=xst[:, b, :]),
                reads=["xs%d_%d" % (s, b)], writes=[], dma=D_O[s]), ndve=4)

        def ffn(t):
            s = t % 2
            xst = xs[s]
            x1T_all = ["x1T_%d" % b for b in range(4)]

            def st_evac(j):
                ws_ = wload(w_up_s[j].rearrange("p k c -> p (k c)"), 2048, "wsc_up%d" % j)
                wv = wsl[ws_][:].rearrange("p (k c) -> p k c", k=8)
                i3 = j % 3
                kG, kV = PA[(2 * j) % 4], PA[(2 * j + 1) % 4]
                for half, kk_ in ((0, kG), (1, kV)):
                    T("pe", lambda e, kk_=kk_, wv=wv, half=half: [
                        e.matmul(psum[kk_][:], lhsT=wv[:, kk, half * 128:(half + 1) * 128], rhs=x1T[:, kk, :],
                                 start=(kk == 0), stop=(kk == 7)) for kk in range(8)],
                      reads=x1T_all + ["wsl%d" % ws_], writes=[pr(kk_)])
                    T("act", lambda e, kk_=kk_, half=half, i3=i3: e.activation(
                        out=hs[i3][:, half, 2:2 + NT], in_=psum[kk_][:], func=AF.Copy),
                      reads=[pr(kk_)], writes=["hs%d_%d" % (i3, half)])

            def st_tap0(j):
                i3 = j % 3
                for half in range(2):
                    ch = j + half * NPAIR
                    T("act", lambda e, half=half, ch=ch: e.activation(
                        out=gt[i3][:, half, :], in_=hs[i3][:, half, 2:2 + NT], func=AF.Identity, bias=cbf[:, ch:ch + 1],
                        scale=cwf[:, ch, 2:3]),
                      reads=["hs%d_%d" % (i3, half), "P"], writes=["gt%d_%d" % (i3, half)])

            def st_taps(j):
                i3 = j % 3
                T("dve", lambda e: e.tensor_copy(out=hs[i3][:, :, 0:2], in_=hsave[:, j, :, :]),
                  reads=["hsave%d" % j], writes=["hsh%d" % i3])
                T("pool", lambda e: e.tensor_copy(out=hsave[:, j, :, :], in_=hs[i3][:, :, NT:NT + 2]),
                  reads=["hs%d_0" % i3, "hs%d_1" % i3], writes=["hsave%d" % j])
                for half in range(2):
                    ch = j + half * NPAIR
                    for tap, off in ((1, 1), (0, 0)):
                        T("dve", lambda e, half=half, ch=ch, tap=tap, off=off: e.scalar_tensor_tensor(
                            out=gt[i3][:, half, :], in0=hs[i3][:, half, off:off + NT], scalar=cwf[:, ch, tap:tap + 1],
                            in1=gt[i3][:, half, :], op0=ALU.mult, op1=ALU.add),
                          reads=["hs%d_%d" % (i3, half), "hsh%d" % i3, "gt%d_%d" % (i3, half), "P"],
                          writes=["gt%d_%d" % (i3, half)])

            def st_sig(j):
                i3, i2 = j % 3, j % 2
                T("act", lambda e: e.activation(out=sgm[i2][:], in_=gt[i3][:, 0, :], func=AF.Sigmoid),
                  reads=["gt%d_0" % i3], writes=["sgm%d" % i2])

            def st_gv(j):
                i3 = j % 3
                T("pool", lambda e: e.tensor_tensor(out=gt[i3][:, 1, :], in0=gt[i3][:, 0, :], in1=gt[i3][:, 1, :], op=ALU.mult),
                  reads=["gt%d_0" % i3, "gt%d_1" % i3], writes=["gt%d_1" % i3])

            def st_fin(j):
                i3, i2 = j % 3, j % 2
                T("pool", lambda e: e.tensor_tensor(out=actT[:, j, :], in0=gt[i3][:, 1, :], in1=sgm[i2][:], op=ALU.mult),
                  reads=["sgm%d" % i2, "gt%d_1" % i3], writes=["actT%d" % j])

            for step in range(NPAIR + 2):
                if step < NPAIR:
                    st_evac(step)
                if 0 <= step - 2 < NPAIR:
                    st_sig(step - 2)
                if step < NPAIR:
                    st_tap0(step)
                if 0 <= step - 2 < NPAIR:
                    st_fin(step - 2)
                if 0 <= step - 1 < NPAIR:
                    st_taps(step - 1)
                    st_gv(step - 1)
                yield "up"
            act_all = ["actT%d" % j for j in range(NPAIR)]
            for gi, (h, c0, ncnk) in enumerate(DN_GROUPS):
                ws_ = wload(w_dn_s[gi].rearrange("p k c -> p (k c)"), 2048, "wsc_dn%d" % gi)
                wv = wsl[ws_][:].rearrange("p (k c) -> p k c", k=4)
                for b in range(4):
                    k = PA[b]
                    T("pe", lambda e, k=k, wv=wv, b=b, c0=c0, ncnk=ncnk: [
                        e.matmul(psum[k][:], lhsT=actT[:, c0 + cl, b * 128:(b + 1) * 128], rhs=wv[:, cl, :],
                                 start=(c0 + cl == 0), stop=(c0 + cl == NPAIR - 1)) for cl in range(ncnk)],
                      reads=["actT%d" % (c0 + cl) for cl in range(ncnk)] + ["wsl%d" % ws_], writes=[pr(k)])
                if c0 + ncnk == NPAIR:
                    for b in range(4):
                        k = PA[b]
                        T("dve", lambda e, k=k, b=b, h=h: e.scalar_tensor_tensor(
                            out=xst[:, b, h * 512:(h + 1) * 512], in0=xst[:, b, h * 512:(h + 1) * 512], scalar=ALPHA,
                            in1=psum[k][:], op0=ALU.mult, op1=ALU.add),
                          reads=[pr(k), "xs%d_%d" % (s, b)], writes=["xs%d_%d" % (s, b)])
                yield "dn"

        def drain(g):
            for _ in g:
                pass

        for g in (6, 4, 7, 5, 0, 1, 2, 3):
            cast_in(g)
        for i in range(4):
            cast_out(i)
        jc = 0
        NEARLY = 10
        for pi_, _ in enumerate(mixer_a(0)):
            if pi_ == 2:
                build_dg32()
                if ntiles > 1:
                    xload_bf(1)
            if jc < NEARLY:
                cast_up(jc)
                jc += 1
        while jc < NEARLY:
            cast_up(jc)
            jc += 1
        drain(mixer_b(0))
        for t in range(ntiles):
            ga = gb = None
            unit = 0
            more = t + 1 < ntiles
            if more:
                ga = mixer_a(t + 1)
                gb = mixer_b(t + 1)
            gf = ffn(t)
            gl = ln2_store(t - 1) if t > 0 else None
            while True:
                S.ctx = "F%d.%d" % (t, unit + 1)
                try:
                    next(gf)
                except StopIteration:
                    break
                unit += 1
                if t == 0:
                    if NEARLY + unit - 1 < NPAIR:
                        cast_up(NEARLY + unit - 1)
                    elif 0 <= unit - 13 < 12:
                        cast_dn(unit - 13)
                if gl is not None:
                    S.ctx = "L%d@%d" % (t - 1, unit)
                    try:
                        next(gl)
                    except StopIteration:
                        gl = None
                if unit == 10 and t + 2 < ntiles:
                    xload_bf(t + 2)
                if not more:
                    continue
                if unit == 12:
                    xload(t + 1)
                if unit > 0:
                    S.ctx = "M%d@%d" % (t + 1, unit)
                    if ga is not None:
                        try:
                            next(ga)
                        except StopIteration:
                            ga = None
                    elif gb is not None and unit > NPAIR + 2:
                        try:
                            next(gb)
                        except StopIteration:
                            gb = None
            S.ctx = "M%d@end" % (t + 1)
            if ga is not None:
                drain(ga)
            if gb is not None:
                drain(gb)
        S.ctx = "L%d@end" % (ntiles - 1)
        drain(ln2_store(ntiles - 1))
        global LAST_LABELS
        LAST_LABELS = S.labels
        S.emit(final_waits=[(("d", D_O[0]), S.dcnt[D_O[0]]), (("d", D_O[1]), S.dcnt[D_O[1]])])
        print("sbuf bytes remaining per partition:", nc.sbuf_bytes_remaining)
    return nc


_PARAM_NAMES = ["w_in", "b_in", "ln_a_g", "ln_a_b", "w_spatial", "b_spatial", "conv_b_w", "conv_b_b", "ln_b_g",
                "ln_b_b", "w_out", "b_out", "ln1_g", "ln1_b", "w_up", "conv_f_w", "conv_f_b", "w_down", "ln2_g", "ln2_b"]


def kernel(**inputs):
    x = np.ascontiguousarray(np.asarray(inputs["x"], dtype=np.float32))
    params = {k: np.ascontiguousarray(np.asarray(inputs[k], dtype=np.float32)) for k in _PARAM_NAMES}
    n = x.shape[0]
    nc = build_nc()
    in_maps = []
    for c in range(n):
        m = {"x": x[c]}
        m.update(params)
        in_maps.append(m)
    res = run_bass_kernel_spmd(nc, in_maps, core_ids=list(range(n)))
    return np.stack([np.asarray(r["out"], dtype=np.float32) for r in res.results], axis=0)
```

```python
import numpy as np
from contextlib import ExitStack
import concourse.bass as bass
import concourse.mybir as mybir
from concourse.bass_utils import run_bass_kernel_spmd

F32 = mybir.dt.float32
BF16 = mybir.dt.bfloat16
AF = mybir.ActivationFunctionType
ALU = mybir.AluOpType

SEQ = 4096
D = 1024
NT = 512
DFF = 2816
NPAIR = 22
EPS = 1e-5
ALPHA = 2.0 ** 0.25
ENGS = ("pe", "act", "dve", "pool", "sp")
N_DMA_SEMS = 28
LAST_LABELS = None


class Sched:
    def __init__(self, nc):
        self.nc = nc
        self.q = {e: [] for e in ENGS}
        self.cnt = {e: 0 for e in ENGS}
        self.dcnt = [0] * N_DMA_SEMS
        self.waited = {e: {} for e in ENGS}
        self.res = {}
        self.ctx = ""
        self.labels = {e: [] for e in ENGS}

    def task(self, eng, fn, reads=(), writes=(), dma=None, extra=()):
        deps = {}

        def add(tok):
            if tok is None:
                return
            k, v = tok
            if deps.get(k, 0) < v:
                deps[k] = v

        for r in reads:
            st = self.res.get(r)
            if st is not None:
                add(st[0])
        for w in writes:
            st = self.res.get(w)
            if st is not None:
                add(st[0])
                for t in st[1]:
                    add(t)
        for t in extra:
            add(t)
        if dma is None:
            self.cnt[eng] += 1
            tok = (eng, self.cnt[eng])
        else:
            self.dcnt[dma] += 16
            tok = (("d", dma), self.dcnt[dma])
        waits = []
        for k, v in deps.items():
            if k == eng and eng in ("pe", "sp"):
                continue
            if self.waited[eng].get(k, 0) >= v:
                continue
            self.waited[eng][k] = v
            waits.append((k, v))
        self.q[eng].append((waits, fn, tok))
        self.labels[eng].append((self.ctx, tok[0] if isinstance(tok[0], str) else "dma%d" % tok[0][1], tok[1], list(writes)[:2]))
        for r in reads:
            st = self.res.setdefault(r, [None, []])
            st[1].append(tok)
        for w in writes:
            self.res[w] = [tok, []]
        return tok

    def emit(self, final_waits=()):
        nc = self.nc
        with ExitStack() as es:
            sems = {}
            for e in ENGS:
                sems[e] = es.enter_context(nc.semaphore("c_" + e))
            for i in range(N_DMA_SEMS):
                sems[("d", i)] = es.enter_context(nc.semaphore("d_%d" % i))
            block = es.enter_context(nc.Block())

            def run(engname, eng):
                for waits, fn, tok in self.q[engname]:
                    for k, v in waits[1:]:
                        eng.wait_ge(sems[k], v)
                    ins = fn(eng)
                    if isinstance(ins, (list, tuple)):
                        first, last = ins[0], ins[-1]
                    else:
                        first = last = ins
                    if waits:
                        first._wait_ge(sems[waits[0][0]], waits[0][1])
                    k, v = tok
                    last.then_inc(sems[k], 1 if k == engname else 16)
                if engname == "sp":
                    for k, v in final_waits:
                        eng.wait_ge(sems[k], v)

            @block.tensor
            def _(eng):
                run("pe", eng)

            @block.scalar
            def _(eng):
                run("act", eng)

            @block.vector
            def _(eng):
                run("dve", eng)

            @block.gpsimd
            def _(eng):
                run("pool", eng)

            @block.sync
            def _(eng):
                run("sp", eng)


def build_nc(ntiles=SEQ // NT):
    nc = bass.Bass("TRN2", target_bir_lowering=False)
    seq = ntiles * NT

    def din(name, shape):
        return nc.dram_tensor(name, list(shape), F32, kind="ExternalInput").ap()

    x = din("x", [seq, D])
    w_in = din("w_in", [D, 2048])
    b_in = din("b_in", [2048])
    ln_a_g = din("ln_a_g", [4, 128])
    ln_a_b = din("ln_a_b", [4, 128])
    w_spatial = din("w_spatial", [4, 128, 128])
    b_spatial = din("b_spatial", [4, 128])
    conv_b_w = din("conv_b_w", [31, 512])
    conv_b_b = din("conv_b_b", [512])
    ln_b_g = din("ln_b_g", [512])
    ln_b_b = din("ln_b_b", [512])
    w_out = din("w_out", [D, D])
    b_out = din("b_out", [D])
    ln1_g = din("ln1_g", [D])
    ln1_b = din("ln1_b", [D])
    w_up = din("w_up", [D, 2 * DFF])
    conv_f_w = din("conv_f_w", [3, 2 * DFF])
    conv_f_b = din("conv_f_b", [2 * DFF])
    w_down = din("w_down", [DFF, D])
    ln2_g = din("ln2_g", [D])
    ln2_b = din("ln2_b", [D])
    out = nc.dram_tensor("out", [seq, D], F32, kind="ExternalOutput").ap()

    w_in_s = nc.dram_tensor("w_in_s", [8, 128, 8, 256], BF16, kind="Internal").ap()
    w_out_s = nc.dram_tensor("w_out_s", [4, 128, 4, 512], BF16, kind="Internal").ap()
    w_up_s = nc.dram_tensor("w_up_s", [NPAIR, 128, 8, 256], BF16, kind="Internal").ap()
    w_dn_s = nc.dram_tensor("w_dn_s", [12, 128, 4, 512], BF16, kind="Internal").ap()

    S = Sched(nc)
    T = S.task
    D_PAR, D_CIN, D_COUT, D_CUP, D_CDN = 0, 1, 2, 3, 4
    D_X = (5, 6)
    D_W = (7, 8, 9, 10)
    D_O = (11, 12)
    D_LR, D_LO = 13, 14
    D_XB = 15

    with ExitStack() as es:
        def sb(name, shape, dt):
            return es.enter_context(nc.sbuf_tensor(name, list(shape), dt))

        ident = sb("ident", [128, 128], BF16)
        identf = sb("identf", [128, 128], F32)
        ones_bf = sb("ones_bf", [128, 128], BF16)
        Dg32 = sb("Dg32", [128, 4, 31, 32], BF16)
        ident32 = sb("ident32", [128, 32], F32)
        wsT = sb("wsT", [128, 4, 128], BF16)
        Rh = sb("Rh", [128, 4, 128], F32)
        cols = sb("cols", [128, 40], F32)
        cwb = sb("cwb", [128, 4, 31], F32)
        cwf = sb("cwf", [128, 44, 3], F32)
        cbf = sb("cbf", [128, 44], F32)
        rows_hl = sb("rows_hl", [2, 1536], BF16)
        ones2 = sb("ones2", [2, 128], BF16)
        wsmb = sb("wsmb", [128, 4, 128], BF16)
        g1t = sb("g1t", [128, D], F32)
        b1t = sb("b1t", [128, D], F32)
        g2t = sb("g2t", [128, D], F32)
        b2t = sb("b2t", [128, D], F32)

        wsl = [sb("wsl%d" % i, [128, 2048], BF16) for i in range(4)]
        xs = [sb("xs%d" % i, [128, 4, D], F32) for i in range(2)]
        xb = [sb("xb%d" % i, [128, D], BF16) for i in range(2)]
        xbx = sb("xbx", [128, 4, D], BF16)
        xT = sb("xT", [128, 8, NT], BF16)
        x1T = sb("x1T", [128, 8, NT], BF16)
        uT = sb("uT", [128, 4, NT], BF16)
        vy = sb("vy", [128, 4, NT], F32)
        vn = sb("vn", [128, 4, NT], BF16)
        sg = [sb("sg%d" % i, [128, NT], F32) for i in range(4)]
        yb = sb("yb", [128, 4, NT + 32], BF16)
        ycb = [sb("ycb%d" % i, [128, NT], BF16) for i in range(2)]
        ycsq = [sb("ycsq%d" % i, [128, NT], BF16) for i in range(2)]
        st_mean = sb("st_mean", [128, NT], F32)
        st_tmp = sb("st_tmp", [128, NT], F32)
        yT = sb("yT", [128, 8, NT], BF16)
        hs = [sb("hs%d" % i, [128, 2, NT + 2], F32) for i in range(3)]
        gt = [sb("gt%d" % i, [128, 2, NT], F32) for i in range(3)]
        sgm = [sb("sgm%d" % i, [128, NT], F32) for i in range(2)]
        hsave = sb("hsave", [128, NPAIR, 2, 2], F32)
        actT = sb("actT", [128, NPAIR, NT], BF16)
        sm = sb("sm", [128, 112], F32)

        psum = [es.enter_context(nc.psum_tensor("ps%d" % i, [128, 512], F32)) for i in range(8)]
        xs1f = xs[1][:].rearrange("p b d -> p (b d)")
        rowsf = xs1f[0:2, 0:1536]
        rows_t = xs1f[0:2, 1536:3072]
        Lr = xs1f[0:2, 3072:3584].rearrange("p (h d) -> p h d", h=4)
        Rr = xs1f[0:2, 3584:4096].rearrange("p (h d) -> p h d", h=4)
        rows_lo = actT[:].rearrange("p j t -> p (j t)")[0:1, 0:1536]
        wsm = gt[0][:].rearrange("p a (h t) -> p (a h) t", h=4)[:, 0:4, :]
        XS1 = ["xs1_%d" % b for b in range(4)]
        st_rstd = st_tmp
        PA = [0, 1, 2, 3]
        PB = [4, 5, 6, 7]
        rrB = [0]

        def nextB():
            k = PB[rrB[0] % 4]
            rrB[0] += 1
            return k

        def pr(k):
            return "ps%d" % k

        DN_GROUPS = []
        for h in range(2):
            for g6 in range(6):
                c0 = g6 * 4
                DN_GROUPS.append((h, c0, min(4, NPAIR - c0)))
        S_CIN = (D_CIN, 22, 23)
        S_CUP = (D_CUP, 16, 17, 18)
        S_CDN = (D_CDN, 19, 20, 21)

        S_COUT = (D_COUT, 24, 25, 26)
        last_on = {}

        def cast_dma(sem, out_ap, in_ap, first_of_group=True):
            extra = [last_on[sem]] if (first_of_group and sem in last_on) else []
            tk = T("pool", lambda e: e.dma_start(out=out_ap, in_=in_ap), dma=sem, extra=extra)
            last_on[sem] = tk
            return tk

        def cast_in(g):
            tk = cast_dma(S_CIN[g % 3], w_in_s[g], w_in[:, g * 256:(g + 1) * 256].rearrange("(k p) c -> p k c", p=128))
            S.res["wsc_in%d" % g] = [tk, []]

        def cast_out(i):
            h, kh = i // 2, i % 2
            tk = cast_dma(S_COUT[i], w_out_s[i],
                          w_out[kh * 512:(kh + 1) * 512, h * 512:(h + 1) * 512].rearrange("(k p) c -> p k c", p=128))
            S.res["wsc_out%d" % i] = [tk, []]

        def cast_up(j):
            cast_dma(S_CUP[j % 4], w_up_s[j, :, :, 0:128],
                     w_up[:, j * 128:(j + 1) * 128].rearrange("(k p) c -> p k c", p=128))
            tk = cast_dma(S_CUP[j % 4], w_up_s[j, :, :, 128:256],
                          w_up[:, DFF + j * 128:DFF + (j + 1) * 128].rearrange("(k p) c -> p k c", p=128),
                          first_of_group=False)
            S.res["wsc_up%d" % j] = [tk, []]

        def cast_dn(gi):
            h, c0, ncnk = DN_GROUPS[gi]
            tk = cast_dma(S_CDN[gi % 4], w_dn_s[gi, :, 0:ncnk, :],
                          w_down[c0 * 128:(c0 + ncnk) * 128, h * 512:(h + 1) * 512].rearrange("(c p) m -> p c m", p=128))
            S.res["wsc_dn%d" % gi] = [tk, []]

        T("pool", lambda e: e.dma_start(out=xbx[:], in_=x[0:NT, :].rearrange("(b p) d -> p b d", p=128)),
          writes=["xbx%d" % b for b in range(4)], dma=D_XB)
        T("sp", lambda e: e.dma_start(out=xs[0][:], in_=x[0:NT, :].rearrange("(b p) d -> p b d", p=128)),
          writes=["xs0_%d" % b for b in range(4)], dma=D_X[0])

        def pload(dst, src):
            return T("sp", lambda e: e.dma_start(out=dst, in_=src, allow_slow_non_contiguous=True), dma=D_PAR)

        stg = hs[1][:].rearrange("p a t -> p (a t)")
        stg2 = hs[2][:].rearrange("p a t -> p (a t)")
        rowsrc = [b_in[0:512], b_in[1536:2048], b_in[1024:1536], None, conv_b_b, ln_b_g, ln_b_b]
        for i_, src_ in enumerate(rowsrc):
            if src_ is None:
                pload(stg[12:16, 0:128], ln_a_g)
            else:
                pload(stg[4 * i_:4 * i_ + 4, 0:128], src_.rearrange("(c p) -> c p", p=128))
        pload(stg[0:44, 128:256], conv_f_b.rearrange("(c p) -> c p", p=128))
        for k in range(3):
            pload(stg[0:44, 256 + 128 * k:384 + 128 * k], conv_f_w[k].rearrange("(c p) -> c p", p=128))
        pload(stg2[0:31, 0:512], conv_b_w)
        pload(rowsf[0:1, 0:512], b_in[512:1024].rearrange("(o n) -> o n", o=1))
        pload(rowsf[0:1, 512:1536], b_out.rearrange("(o n) -> o n", o=1))
        pload(Rr[1:2, :, :], b_spatial.rearrange("(o h) t -> o h t", o=1))
        t_par = pload(wsm[:, :, :], w_spatial.rearrange("h t s -> t h s"))
        T("pool", lambda e: e.memset(Lr[:], 1.0), writes=["Lr"])
        t_lr = T("sp", lambda e: e.dma_start(out=Lr[0:1, :, :], in_=ln_a_b.rearrange("(o h) d -> o h d", o=1)),
                 reads=["Lr"], dma=D_LR)
        t_big = None
        for dst_, src_ in ((g1t, ln1_g), (b1t, ln1_b), (g2t, ln2_g), (b2t, ln2_b)):
            t_big = T("sp", lambda e, dst_=dst_, src_=src_: e.dma_start(out=dst_[:, :], in_=src_.partition_broadcast(128)),
                      dma=27)
        S.res["Pbig"] = [t_big, []]
        S.res["P"] = [t_par, []]

        T("pool", lambda e: e.memset(identf[:], 0.0), writes=["identf"])
        T("pool", lambda e: e.affine_select(out=identf[:], in_=identf[:], pattern=[[-1, 128]],
                                            compare_op=ALU.not_equal, fill=1.0, base=0, channel_multiplier=1),
          reads=["identf"], writes=["identf"])
        T("pool", lambda e: e.tensor_copy(out=ident[:], in_=identf[:]), reads=["identf"], writes=["ident"])
        T("pool", lambda e: e.memset(ones_bf[:], 1.0), writes=["ones_bf"])
        T("pool", lambda e: e.memset(ones2[:], 1.0), writes=["ones2"])
        T("pool", lambda e: e.memset(hsave[:], 0.0), writes=["hsave"])
        T("pool", lambda e: e.memset(yb[:], 0.0), writes=["yb0", "yb1", "yb2", "yb3"])

        T("dve", lambda e: e.tensor_copy(out=rows_hl[0:1, :], in_=rowsf[0:1, :]), reads=["P"], writes=["rows_hi"])
        T("dve", lambda e: e.tensor_copy(out=rows_t[0:1, :], in_=rows_hl[0:1, :]), reads=["rows_hi"], writes=["rows_t"])
        T("dve", lambda e: e.tensor_tensor(out=rows_t[0:1, :], in0=rowsf[0:1, :], in1=rows_t[0:1, :], op=ALU.subtract),
          reads=["rows_t", "P"], writes=["rows_t"])
        T("dve", lambda e: e.tensor_copy(out=rows_lo[0:1, :], in_=rows_t[0:1, :]), reads=["rows_t"], writes=["rows_lo"])
        T("sp", lambda e: e.dma_start(out=rows_hl[1:2, :], in_=rows_lo[0:1, :]), reads=["rows_lo", "rows_hi"],
          writes=["rows_hl"], dma=D_LO)

        pkp = nextB()

        def par_transposes(e):
            o = []
            pp = psum[pkp]
            o.append(e.transpose(out=pp[:, 0:28], in_=stg[0:28, 0:128], identity=identf[0:28, 0:28]))
            o.append(e.transpose(out=pp[:, 32:76], in_=stg[0:44, 128:256], identity=identf[0:44, 0:44]))
            for k in range(3):
                o.append(e.transpose(out=pp[:, 80 + 48 * k:124 + 48 * k], in_=stg[0:44, 256 + 128 * k:384 + 128 * k],
                                     identity=identf[0:44, 0:44]))
            for c in range(4):
                o.append(e.transpose(out=pp[:, 224 + 32 * c:255 + 32 * c], in_=stg2[0:31, c * 128:(c + 1) * 128],
                                     identity=identf[0:31, 0:31]))
            return o
        T("pe", par_transposes, reads=["P", "identf"], writes=[pr(pkp)])
        T("dve", lambda e: e.tensor_copy(out=cols[:, 0:28], in_=psum[pkp][:, 0:28]), reads=[pr(pkp)], writes=["cols"])
        T("dve", lambda e: e.tensor_copy(out=cbf[:, :], in_=psum[pkp][:, 32:76]), reads=[pr(pkp)], writes=["cbf"])
        T("dve", lambda e: e.tensor_copy(out=cwf[:].rearrange("p c k -> p k c"),
                                         in_=psum[pkp][:, 80:224].rearrange("p (k c) -> p k c", k=3)[:, :, 0:44]),
          reads=[pr(pkp)], writes=["cwf"])
        T("dve", lambda e: e.tensor_copy(out=cwb[:],
                                         in_=psum[pkp][:, 224:352].rearrange("p (c k) -> p c k", c=4)[:, :, 0:31]),
          reads=[pr(pkp)], writes=["cwb"])
        S.res["P"] = [("dve", S.cnt["dve"]), []]
        S.res["Pdma"] = [t_par, []]

        for h in range(4):
            T("pool", lambda e, h=h: e.affine_select(out=wsm[:, h, :], in_=wsm[:, h, :], pattern=[[-1, 128]],
                                                     compare_op=ALU.is_ge, fill=0.0, base=0, channel_multiplier=1),
              reads=["P"], writes=["wsm%d" % h])
            T("pool", lambda e, h=h: e.tensor_copy(out=wsmb[:, h, :], in_=wsm[:, h, :]), reads=["wsm%d" % h],
              writes=["wsmb%d" % h])
        pk = nextB()
        T("pe", lambda e: [e.transpose(out=psum[pk][:].bitcast(BF16)[:, h * 128:(h + 1) * 128], in_=wsmb[:, h, :],
                                       identity=ident[:]) for h in range(4)],
          reads=["wsmb0", "wsmb1", "wsmb2", "wsmb3", "ident"], writes=[pr(pk)])
        T("dve", lambda e: e.tensor_copy(out=wsT[:].rearrange("p h t -> p (h t)"), in_=psum[pk][:].bitcast(BF16)[:, 0:512]),
          reads=[pr(pk)], writes=["wsT"])
        pk2 = nextB()
        T("pe", lambda e: [e.matmul(psum[pk2][0:1, h * 128:(h + 1) * 128], lhsT=ones_bf[:, 0:1], rhs=wsT[:, h, :],
                                    start=True, stop=True) for h in range(4)],
          reads=["wsT", "ones_bf"], writes=[pr(pk2)])
        T("dve", lambda e: e.tensor_copy(out=Rr[0:1, :, :].rearrange("p h t -> p (h t)"), in_=psum[pk2][0:1, :]),
          reads=[pr(pk2), "P"], writes=["Rr"])
        pk3 = nextB()
        T("pe", lambda e: [e.matmul(psum[pk3][:, h * 128:(h + 1) * 128], lhsT=Lr[:, h, :], rhs=Rr[:, h, :],
                                    start=True, stop=True) for h in range(4)],
          reads=["Rr", "Lr", "P"], writes=[pr(pk3)], extra=[t_lr])
        T("dve", lambda e: e.tensor_copy(out=Rh[:].rearrange("p h t -> p (h t)"), in_=psum[pk3][:]),
          reads=[pr(pk3)], writes=["Rh"])

        def build_dg32():
            T("dve", lambda e: e.tensor_tensor(out=ident32[:], in0=identf[:, 0:32], in1=identf[:, 32:64], op=ALU.add),
              reads=["identf"], writes=["ident32"])
            T("dve", lambda e: e.tensor_tensor(out=ident32[:], in0=ident32[:], in1=identf[:, 64:96], op=ALU.add),
              reads=["identf", "ident32"], writes=["ident32"])
            T("dve", lambda e: e.tensor_tensor(out=ident32[:], in0=ident32[:], in1=identf[:, 96:128], op=ALU.add),
              reads=["identf", "ident32"], writes=["ident32"])
            for c in range(4):
                for k in range(31):
                    T("dve", lambda e, c=c, k=k: e.tensor_scalar(out=Dg32[:, c, k, :], in0=ident32[:], scalar1=cwb[:, c, k:k + 1],
                                                                 scalar2=None, op0=ALU.mult),
                      reads=["ident32", "P"], writes=["Dg%d_%d" % (c, k)])


        prep_toks = [(e_, S.cnt[e_]) for e_ in ("pe", "act", "dve", "pool") if S.cnt[e_] > 0]
        prep_toks += [(("d", D_PAR), S.dcnt[D_PAR]), (("d", D_LR), S.dcnt[D_LR]), (("d", D_LO), S.dcnt[D_LO])]
        for r_ in XS1 + ["actT0", "actT1", "actT2", "gt0_0", "gt0_1", "hs1_0", "hs1_1", "hs2_0", "hs2_1", "hsh1", "hsh2"]:
            S.res.setdefault(r_, [None, []])[1].extend(prep_toks)

        wcount = [0]

        def wload(src_ap, ncols, scr_res):
            i = wcount[0]
            wcount[0] += 1
            s = i % 4
            T("sp", lambda e: e.dma_start(out=wsl[s][:, 0:ncols], in_=src_ap), reads=[scr_res], writes=["wsl%d" % s],
              dma=D_W[s])
            return s

        def xload_bf(t):
            T("pool", lambda e: e.dma_start(out=xbx[:], in_=x[t * NT:(t + 1) * NT, :].rearrange("(b p) d -> p b d", p=128)),
              writes=["xbx%d" % b for b in range(4)], dma=D_XB)

        def xload(t):
            s = t % 2
            T("sp", lambda e: e.dma_start(out=xs[s][:], in_=x[t * NT:(t + 1) * NT, :].rearrange("(b p) d -> p b d", p=128)),
              writes=["xs%d_%d" % (s, b) for b in range(4)], dma=D_X[s])

        INV_SQRT2 = 0.7071067811865476
        cols2 = sb("cols2", [128, 8], F32)
        T("dve", lambda e: e.tensor_scalar(out=cols2[:, 0:4], in0=cols[:, 0:4], scalar1=INV_SQRT2, scalar2=None,
                                           op0=ALU.mult), reads=["P"], writes=["cols2a"])
        T("dve", lambda e: e.tensor_scalar(out=cols2[:, 4:8], in0=cols[:, 0:4], scalar1=0.5, scalar2=None,
                                           op0=ALU.mult), reads=["P"], writes=["cols2b"])
        sm2 = sb("sm2", [128, 40], F32)
        epsc = sb("epsc", [128, 1], F32)
        T("pool", lambda e: e.memset(epsc[:], EPS), writes=["epsc"])
        rrS = [0]

        def nextS():
            i = rrS[0] % len(sg)
            rrS[0] += 1
            return i

        def transposes(src_tile, dstT, s, tagr, from_bf=False):
            for b in range(4):
                xbi = b % 2
                if from_bf:
                    srcb, sres = xbx[:, b, :], "xbx%d" % b
                else:
                    T("act", lambda e, b=b, xbi=xbi: e.activation(out=xb[xbi][:], in_=src_tile[:, b, :], func=AF.Copy),
                      reads=["xs%d_%d" % (s, b)], writes=["xb%d" % xbi])
                    srcb, sres = xb[xbi][:], "xb%d" % xbi
                k = nextB()
                T("pe", lambda e, k=k, srcb=srcb: [e.transpose(out=psum[k][:].bitcast(BF16)[:, c * 128:(c + 1) * 128],
                                                               in_=srcb[:, c * 128:(c + 1) * 128], identity=ident[:])
                                                   for c in range(8)],
                  reads=[sres, "ident"], writes=[pr(k)])
                T("dve", lambda e, k=k, b=b: e.tensor_copy(
                    out=dstT[:, :, b * 128:(b + 1) * 128],
                    in_=psum[k][:].bitcast(BF16).rearrange("p (c t) -> p c t", c=8)),
                  reads=[pr(k)], writes=["%s_%d" % (tagr, b)])
                if b % 2 == 1:
                    yield

        def gelu_from_psum(k, out_ap, out_res, bias_s=None, bias_h=None):
            ia, ib = nextS(), nextS()
            if bias_s is None:
                T("act", lambda e: e.activation(out=sg[ia][:], in_=psum[k][:], func=AF.Erf, scale=INV_SQRT2),
                  reads=[pr(k)], writes=["sg%d" % ia])
                T("act", lambda e: e.activation(out=sg[ib][:], in_=psum[k][:], func=AF.Identity, scale=0.5),
                  reads=[pr(k)], writes=["sg%d" % ib])
            else:
                T("act", lambda e: e.activation(out=sg[ia][:], in_=psum[k][:], func=AF.Erf, scale=INV_SQRT2, bias=bias_s),
                  reads=[pr(k), "cols2a"], writes=["sg%d" % ia])
                T("act", lambda e: e.activation(out=sg[ib][:], in_=psum[k][:], func=AF.Identity, scale=0.5, bias=bias_h),
                  reads=[pr(k), "cols2b"], writes=["sg%d" % ib])
            T("dve", lambda e: e.scalar_tensor_tensor(out=out_ap, in0=sg[ia][:], scalar=1.0, in1=sg[ib][:],
                                                      op0=ALU.add, op1=ALU.mult),
              reads=["sg%d" % ia, "sg%d" % ib], writes=[out_res])

        def mixer_a(t):
            s = t % 2
            xst = xs[s]
            xT_all = ["xT_%d" % b for b in range(4)]
            kM, kQ = PB[2], PB[3]

            def inproj_fm(wv, cc, ws_):
                k = nextB()
                T("pe", lambda e: [
                    e.matmul(psum[k][:], lhsT=wv[:, kk, cc * 128:(cc + 1) * 128], rhs=xT[:, kk, :],
                             start=(kk == 0), stop=(kk == 7)) for kk in range(8)],
                  reads=xT_all + ["wsl%d" % ws_], writes=[pr(k)])
                return k

            def stats_mm(c):
                T("pe", lambda e: e.matmul(psum[kM][:], lhsT=ones_bf[:], rhs=ycb[c % 2][:], start=(c == 0), stop=(c == 3)),
                  reads=["ycb%d" % (c % 2), "ones_bf"], writes=[pr(kM)])
                T("pe", lambda e: e.matmul(psum[kQ][:], lhsT=ones_bf[:], rhs=ycsq[c % 2][:], start=(c == 0), stop=(c == 3)),
                  reads=["ycsq%d" % (c % 2), "ones_bf"], writes=[pr(kQ)])

            def lnb_stats():
                T("dve", lambda e: e.tensor_scalar(out=st_mean[:], in0=psum[kM][:], scalar1=1.0 / 512, scalar2=None,
                                                   op0=ALU.mult), reads=[pr(kM)], writes=["st_mean"])
                T("dve", lambda e: e.tensor_tensor(out=st_tmp[:], in0=st_mean[:], in1=st_mean[:], op=ALU.mult),
                  reads=["st_mean"], writes=["st_tmp"])
                T("dve", lambda e: e.scalar_tensor_tensor(out=st_tmp[:], in0=psum[kQ][:], scalar=1.0 / 512, in1=st_tmp[:],
                                                          op0=ALU.mult, op1=ALU.subtract),
                  reads=[pr(kQ), "st_tmp"], writes=["st_tmp"])
                T("act", lambda e: e.activation(out=st_tmp[:], in_=st_tmp[:], func=AF.Ln, bias=epsc[:, 0:1], scale=1.0),
                  reads=["st_tmp", "epsc"], writes=["st_tmp"])
                T("act", lambda e: e.activation(out=st_tmp[:], in_=st_tmp[:], func=AF.Exp, scale=-0.5),
                  reads=["st_tmp"], writes=["st_tmp"])

            def lnb_chunk(c):
                si, sj = nextS(), nextS()
                T("dve", lambda e: e.tensor_tensor(out=sg[si][:], in0=vy[:, c, :], in1=st_mean[:], op=ALU.subtract),
                  reads=["vy%d" % c, "st_mean"], writes=["sg%d" % si])
                T("dve", lambda e: e.scalar_tensor_tensor(out=sg[si][:], in0=sg[si][:], scalar=cols[:, 20 + c:21 + c],
                                                          in1=st_tmp[:], op0=ALU.mult, op1=ALU.mult),
                  reads=["sg%d" % si, "st_tmp", "P"], writes=["sg%d" % si])
                T("act", lambda e: e.activation(out=sg[sj][:], in_=sg[si][:], func=AF.Sigmoid,
                                                bias=cols[:, 24 + c:25 + c], scale=1.0),
                  reads=["sg%d" % si, "P"], writes=["sg%d" % sj])
                T("dve", lambda e: e.scalar_tensor_tensor(out=yT[:, 4 + c, :], in0=sg[si][:], scalar=cols[:, 24 + c:25 + c],
                                                          in1=sg[sj][:], op0=ALU.add, op1=ALU.mult),
                  reads=["sg%d" % si, "sg%d" % sj, "P"], writes=["yT%d" % (4 + c)])

            yield from transposes(xst, xT, s, "xT", from_bf=True)
            yield
            if t > 0:
                T("pool", lambda e: e.tensor_copy(out=yb[:, :, 2:32], in_=yb[:, :, NT + 2:NT + 32]),
                  reads=["yb%d" % c for c in range(4)], writes=["yb%d" % c for c in range(4)])
            for g in range(2):
                ws_ = wload(w_in_s[6 + g].rearrange("p k c -> p (k c)"), 2048, "wsc_in%d" % (6 + g))
                wv = wsl[ws_][:].rearrange("p (k c) -> p k c", k=8)
                sidx = {}
                for cc in range(2):
                    c = g * 2 + cc
                    k = inproj_fm(wv, cc, ws_)
                    si = nextS()
                    sidx[c] = si
                    T("act", lambda e, k=k, c=c, si=si: e.activation(out=sg[si][:], in_=psum[k][:], func=AF.Sigmoid,
                                                                    bias=cols[:, 4 + c:5 + c], scale=1.0),
                      reads=[pr(k), "P"], writes=["sg%d" % si])
                yield
                ws2 = wload(w_in_s[4 + g].rearrange("p k c -> p (k c)"), 2048, "wsc_in%d" % (4 + g))
                wv2 = wsl[ws2][:].rearrange("p (k c) -> p k c", k=8)
                for cc in range(2):
                    c = g * 2 + cc
                    k = inproj_fm(wv2, cc, ws2)
                    si = sidx[c]
                    T("dve", lambda e, k=k, c=c, si=si: e.scalar_tensor_tensor(
                        out=yb[:, c, 32:32 + NT], in0=psum[k][:], scalar=cols[:, 8 + c:9 + c], in1=sg[si][:],
                        op0=ALU.add, op1=ALU.mult),
                      reads=[pr(k), "sg%d" % si, "P"], writes=["yb%d" % c])
                yield
            for c in range(4):
                k = PB[c % 2]
                T("pe", lambda e, k=k, c=c: [
                    e.matmul(psum[k][32 * q:32 * q + 32, :], lhsT=Dg32[32 * q:32 * q + 32, c, kk, :],
                             rhs=yb[32 * q:32 * q + 32, c, 2 + kk:2 + kk + NT],
                             start=(kk == 0), stop=(kk == 30), tile_position=(32 * q, 32 * q))
                    for kk in range(31) for q in range(4)],
                  reads=["yb%d" % c] + ["Dg%d_%d" % (c, kk) for kk in range(31)], writes=[pr(k)])
                T("act", lambda e, k=k, c=c: e.activation(out=ycb[c % 2][:], in_=psum[k][:], func=AF.Identity,
                                                          bias=cols[:, 16 + c:17 + c], scale=1.0),
                  reads=[pr(k), "P"], writes=["ycb%d" % (c % 2)])
                T("act", lambda e, k=k, c=c: e.activation(out=ycsq[c % 2][:], in_=psum[k][:], func=AF.Square,
                                                          bias=cols[:, 16 + c:17 + c], scale=1.0),
                  reads=[pr(k), "P"], writes=["ycsq%d" % (c % 2)])
                T("act", lambda e, k=k, c=c: e.activation(out=vy[:, c, :], in_=psum[k][:], func=AF.Identity,
                                                          bias=cols[:, 16 + c:17 + c], scale=1.0),
                  reads=[pr(k), "P"], writes=["vy%d" % c])
                if c > 0:
                    stats_mm(c - 1)
                yield
            for g in range(2):
                ws_ = wload(w_in_s[g].rearrange("p k c -> p (k c)"), 2048, "wsc_in%d" % g)
                wv = wsl[ws_][:].rearrange("p (k c) -> p k c", k=8)
                for cc in range(2):
                    c = g * 2 + cc
                    k = inproj_fm(wv, cc, ws_)
                    gelu_from_psum(k, uT[:, c, :], "uT%d" % c, cols2[:, c:c + 1], cols2[:, 4 + c:5 + c])
                if g == 0:
                    stats_mm(3)
                    lnb_stats()
                if g == 1:
                    lnb_chunk(0)
                yield
            vbanks = [PB[2], PB[3], PB[0], PB[1]]
            for g in range(2):
                ws_ = wload(w_in_s[2 + g].rearrange("p k c -> p (k c)"), 2048, "wsc_in%d" % (2 + g))
                wv = wsl[ws_][:].rearrange("p (k c) -> p k c", k=8)
                for b in range(4):
                    k = vbanks[b]
                    T("pe", lambda e, k=k, wv=wv, b=b, g=g: [
                        e.matmul(psum[k][:, g * 256:(g + 1) * 256], lhsT=xT[:, kk, b * 128:(b + 1) * 128],
                                 rhs=wv[:, kk, :], start=(kk == 0), stop=False) for kk in range(8)] + [
                        e.matmul(psum[k][:, g * 256:(g + 1) * 256], lhsT=ones2[:, :],
                                 rhs=rows_hl[:, g * 256:(g + 1) * 256], start=False, stop=True)],
                      reads=["xT_%d" % b, "wsl%d" % ws_, "rows_hl", "ones2"], writes=[pr(k)])
                lnb_chunk(1 + g)
                yield

            def v_block(b):
                k = vbanks[b]
                gelu_from_psum(k, vy[:, b, :], "vy%d" % b)
                for h in range(4):
                    T("dve", lambda e, b=b, h=h: e.bn_stats(out=sm[:, h * 6:h * 6 + 6],
                                                            in_=vy[:, b, h * 128:(h + 1) * 128]),
                      reads=["vy%d" % b], writes=["smst%d" % h])
                    i16 = b * 4 + h
                    T("dve", lambda e, h=h, i16=i16: e.bn_aggr(out=sm[:, 32 + i16 * 2:34 + i16 * 2],
                                                               in_=sm[:, h * 6:h * 6 + 6]),
                      reads=["smst%d" % h], writes=["smmv%d" % i16])

            v_block(0)
            v_block(1)
            lnb_chunk(3)
            yield
            v_block(2)
            v_block(3)
            mvv = sm[:, 32:64].rearrange("p (i t) -> p i t", t=2)
            allmv = ["smmv%d" % i for i in range(16)]
            T("act", lambda e: e.activation(out=sm[:, 96:112], in_=mvv[:, :, 1], func=AF.Ln, bias=epsc[:, 0:1], scale=1.0),
              reads=allmv + ["epsc"], writes=["smln"])
            T("act", lambda e: e.activation(out=sm[:, 64:80], in_=sm[:, 96:112], func=AF.Exp, scale=-0.5),
              reads=["smln"], writes=["smrs"])
            T("dve", lambda e: e.scalar_tensor_tensor(out=sm[:, 80:96], in0=mvv[:, :, 0], scalar=-1.0, in1=sm[:, 64:80],
                                                      op0=ALU.mult, op1=ALU.mult),
              reads=allmv + ["smrs"], writes=["smnb"])
            for b in range(4):
                for h in range(4):
                    i16 = b * 4 + h
                    T("dve", lambda e, b=b, h=h, i16=i16: e.tensor_scalar(
                        out=vn[:, b, h * 128:(h + 1) * 128], in0=vy[:, b, h * 128:(h + 1) * 128],
                        scalar1=sm[:, 64 + i16:65 + i16], scalar2=sm[:, 80 + i16:81 + i16], op0=ALU.mult, op1=ALU.add),
                      reads=["vy%d" % b, "smrs", "smnb"], writes=["vn%d_%d" % (b, h)])
            yield
            yield
            yield
            for h in range(4):
                k = nextB()
                si = nextS()
                T("pe", lambda e, k=k, h=h: [
                    e.matmul(psum[k][:, b * 128:(b + 1) * 128], lhsT=vn[:, b, h * 128:(h + 1) * 128], rhs=wsT[:, h, :],
                             start=True, stop=True) for b in range(4)],
                  reads=["vn%d_%d" % (b, h) for b in range(4)] + ["wsT"], writes=[pr(k)])
                T("dve", lambda e, k=k, h=h, si=si: e.scalar_tensor_tensor(
                    out=sg[si][:].rearrange("p (b t) -> p b t", b=4),
                    in0=psum[k][:].rearrange("p (b t) -> p b t", b=4), scalar=cols[:, 12 + h:13 + h],
                    in1=Rh[:, h:h + 1, :].broadcast_to([128, 4, 128]), op0=ALU.mult, op1=ALU.add),
                  reads=[pr(k), "Rh", "P"], writes=["sg%d" % si])
                T("dve" if t == 0 else "pool",
                  lambda e, h=h, si=si: e.tensor_tensor(out=yT[:, h, :], in0=sg[si][:], in1=uT[:, h, :], op=ALU.mult),
                  reads=["sg%d" % si, "uT%d" % h], writes=["yT%d" % h])
            yield
            yield
            yield
            yT_all = ["yT%d" % c for c in range(8)]
            for h in range(2):
                for ki, kh in enumerate((1, 0)):
                    ws_ = wload(w_out_s[h * 2 + kh].rearrange("p k c -> p (k c)"), 2048, "wsc_out%d" % (h * 2 + kh))
                    wv = wsl[ws_][:].rearrange("p (k c) -> p k c", k=4)
                    for b in range(4):
                        k = PB[b]
                        T("pe", lambda e, k=k, wv=wv, b=b, kh=kh, h=h, ki=ki: [
                            e.matmul(psum[k][:], lhsT=yT[:, kh * 4 + kk, b * 128:(b + 1) * 128], rhs=wv[:, kk, :],
                                     start=(ki == 0 and kk == 0), stop=False) for kk in range(4)] + ([
                            e.matmul(psum[k][:], lhsT=ones2[:, :], rhs=rows_hl[:, 512 + h * 512:512 + (h + 1) * 512],
                                     start=False, stop=True)] if ki == 1 else []),
                          reads=["yT%d" % (kh * 4 + kk) for kk in range(4)] + ["wsl%d" % ws_, "rows_hl", "ones2"],
                          writes=[pr(k)])
                    if ki == 1:
                        for b in range(4):
                            k = PB[b]
                            T("dve", lambda e, k=k, b=b, h=h: e.scalar_tensor_tensor(
                                out=xst[:, b, h * 512:(h + 1) * 512], in0=xst[:, b, h * 512:(h + 1) * 512], scalar=ALPHA,
                                in1=psum[k][:], op0=ALU.mult, op1=ALU.add),
                              reads=[pr(k), "xs%d_%d" % (s, b)], writes=["xs%d_%d" % (s, b)])
                        if h == 0:
                            yield
                    yield
            yield from layernorm_gen(xst, s, g1t, b1t, ndve=(4 if t == 0 else 3))

        def mixer_b(t):
            s = t % 2
            src_tile = xs[s]

            def cast(b):
                T("act", lambda e: e.activation(out=xb[b % 2][:], in_=src_tile[:, b, :], func=AF.Copy),
                  reads=["xs%d_%d" % (s, b)], writes=["xb%d" % (b % 2)])
            cast(0)
            yield
            for b in range(4):
                if b + 1 < 4:
                    cast(b + 1)
                k = nextB()
                srcb = xb[b % 2][:]
                T("pe", lambda e, k=k, srcb=srcb: [e.transpose(out=psum[k][:].bitcast(BF16)[:, c * 128:(c + 1) * 128],
                                                               in_=srcb[:, c * 128:(c + 1) * 128], identity=ident[:])
                                                   for c in range(8)],
                  reads=["xb%d" % (b % 2), "ident"], writes=[pr(k)])
                T("dve", lambda e, k=k, b=b: e.tensor_copy(
                    out=x1T[:, :, b * 128:(b + 1) * 128],
                    in_=psum[k][:].bitcast(BF16).rearrange("p (c t) -> p c t", c=8)),
                  reads=[pr(k)], writes=["x1T_%d" % b])
                yield

        def layernorm_gen(xst, s, gt_, bt_, after=None, ndve=2):
            rns = ["xs%d_%d" % (s, b) for b in range(4)]
            for b in range(4):
                T("dve", lambda e, b=b: e.bn_stats(out=sm2[:, 0:6], in_=xst[:, b, 0:512]), reads=[rns[b]], writes=["sm2a"])
                T("dve", lambda e, b=b: e.bn_stats(out=sm2[:, 6:12], in_=xst[:, b, 512:1024]), reads=[rns[b]], writes=["sm2b"])
                T("dve", lambda e, b=b: e.bn_aggr(out=sm2[:, 16 + 2 * b:18 + 2 * b], in_=sm2[:, 0:12]),
                  reads=["sm2a", "sm2b"], writes=["sm2mv%d" % b])
            yield
            mv4 = sm2[:, 16:24].rearrange("p (b t) -> p b t", t=2)
            allmv = ["sm2mv%d" % b for b in range(4)]
            T("act", lambda e: e.activation(out=sm2[:, 24:28], in_=mv4[:, :, 1], func=AF.Ln, bias=epsc[:, 0:1], scale=1.0),
              reads=allmv + ["epsc"], writes=["sm2ln"])
            T("act", lambda e: e.activation(out=sm2[:, 28:32], in_=sm2[:, 24:28], func=AF.Exp, scale=-0.5),
              reads=["sm2ln"], writes=["sm2rs"])
            yield
            for b in range(4):
                T("dve", lambda e, b=b: e.scalar_tensor_tensor(out=xst[:, b, :], in0=xst[:, b, :],
                                                                scalar=sm2[:, 16 + 2 * b:17 + 2 * b], in1=gt_[:],
                                                                op0=ALU.subtract, op1=ALU.mult),
                  reads=[rns[b], "sm2mv%d" % b, "Pbig"], writes=[rns[b]])
                T("dve", lambda e, b=b: e.scalar_tensor_tensor(out=xst[:, b, :], in0=xst[:, b, :],
                                                                scalar=sm2[:, 28 + b:29 + b], in1=bt_[:],
                                                                op0=ALU.mult, op1=ALU.add),
                  reads=[rns[b], "sm2rs", "Pbig"], writes=[rns[b]])
                if after is not None:
                    after(b)
                if b % 2 == 1:
                    yield

        def layernorm_all(xst, s, gt_, bt_, after=None):
            for _ in layernorm_gen(xst, s, gt_, bt_, after):
                pass

        def ln2_store(t):
            s = t % 2
            xst = xs[s]
            yield from layernorm_gen(xst, s, g2t, b2t, after=lambda b: T(
                "pool", lambda e, b=b: e.dma_start(out=out[t * NT + b * 128:t * NT + (b + 1) * 128, :], in_=xst[:, b, :]),
                reads=["xs%d_%d" % (s, b)], writes=[], dma=D_O[s]), ndve=4)

        def ffn(t):
            s = t % 2
            xst = xs[s]
            x1T_all = ["x1T_%d" % b for b in range(4)]

            def st_evac(j):
                ws_ = wload(w_up_s[j].rearrange("p k c -> p (k c)"), 2048, "wsc_up%d" % j)
                wv = wsl[ws_][:].rearrange("p (k c) -> p k c", k=8)
                i3 = j % 3
                kG, kV = PA[(2 * j) % 4], PA[(2 * j + 1) % 4]
                for half, kk_ in ((0, kG), (1, kV)):
                    T("pe", lambda e, kk_=kk_, wv=wv, half=half: [
                        e.matmul(psum[kk_][:], lhsT=wv[:, kk, half * 128:(half + 1) * 128], rhs=x1T[:, kk, :],
                                 start=(kk == 0), stop=(kk == 7)) for kk in range(8)],
                      reads=x1T_all + ["wsl%d" % ws_], writes=[pr(kk_)])
                    T("act", lambda e, kk_=kk_, half=half, i3=i3: e.activation(
                        out=hs[i3][:, half, 2:2 + NT], in_=psum[kk_][:], func=AF.Copy),
                      reads=[pr(kk_)], writes=["hs%d_%d" % (i3, half)])

            def st_tap0(j):
                i3 = j % 3
                for half in range(2):
                    ch = j + half * NPAIR
                    T("act", lambda e, half=half, ch=ch: e.activation(
                        out=gt[i3][:, half, :], in_=hs[i3][:, half, 2:2 + NT], func=AF.Identity, bias=cbf[:, ch:ch + 1],
                        scale=cwf[:, ch, 2:3]),
                      reads=["hs%d_%d" % (i3, half), "P"], writes=["gt%d_%d" % (i3, half)])

            def st_taps(j):
                i3 = j % 3
                T("dve", lambda e: e.tensor_copy(out=hs[i3][:, :, 0:2], in_=hsave[:, j, :, :]),
                  reads=["hsave%d" % j], writes=["hsh%d" % i3])
                T("pool", lambda e: e.tensor_copy(out=hsave[:, j, :, :], in_=hs[i3][:, :, NT:NT + 2]),
                  reads=["hs%d_0" % i3, "hs%d_1" % i3], writes=["hsave%d" % j])
                for half in range(2):
                    ch = j + half * NPAIR
                    for tap, off in ((1, 1), (0, 0)):
                        T("dve", lambda e, half=half, ch=ch, tap=tap, off=off: e.scalar_tensor_tensor(
                            out=gt[i3][:, half, :], in0=hs[i3][:, half, off:off + NT], scalar=cwf[:, ch, tap:tap + 1],
                            in1=gt[i3][:, half, :], op0=ALU.mult, op1=ALU.add),
                          reads=["hs%d_%d" % (i3, half), "hsh%d" % i3, "gt%d_%d" % (i3, half), "P"],
                          writes=["gt%d_%d" % (i3, half)])

            def st_sig(j):
                i3, i2 = j % 3, j % 2
                T("act", lambda e: e.activation(out=sgm[i2][:], in_=gt[i3][:, 0, :], func=AF.Sigmoid),
                  reads=["gt%d_0" % i3], writes=["sgm%d" % i2])

            def st_gv(j):
                i3 = j % 3
                T("pool", lambda e: e.tensor_tensor(out=gt[i3][:, 1, :], in0=gt[i3][:, 0, :], in1=gt[i3][:, 1, :], op=ALU.mult),
                  reads=["gt%d_0" % i3, "gt%d_1" % i3], writes=["gt%d_1" % i3])

            def st_fin(j):
                i3, i2 = j % 3, j % 2
                T("pool", lambda e: e.tensor_tensor(out=actT[:, j, :], in0=gt[i3][:, 1, :], in1=sgm[i2][:], op=ALU.mult),
                  reads=["sgm%d" % i2, "gt%d_1" % i3], writes=["actT%d" % j])

            for step in range(NPAIR + 2):
                if step < NPAIR:
                    st_evac(step)
                if 0 <= step - 2 < NPAIR:
                    st_sig(step - 2)
                if step < NPAIR:
                    st_tap0(step)
                if 0 <= step - 2 < NPAIR:
                    st_fin(step - 2)
                if 0 <= step - 1 < NPAIR:
                    st_taps(step - 1)
                    st_gv(step - 1)
                yield "up"
            act_all = ["actT%d" % j for j in range(NPAIR)]
            for gi, (h, c0, ncnk) in enumerate(DN_GROUPS):
                ws_ = wload(w_dn_s[gi].rearrange("p k c -> p (k c)"), 2048, "wsc_dn%d" % gi)
                wv = wsl[ws_][:].rearrange("p (k c) -> p k c", k=4)
                for b in range(4):
                    k = PA[b]
                    T("pe", lambda e, k=k, wv=wv, b=b, c0=c0, ncnk=ncnk: [
                        e.matmul(psum[k][:], lhsT=actT[:, c0 + cl, b * 128:(b + 1) * 128], rhs=wv[:, cl, :],
                                 start=(c0 + cl == 0), stop=(c0 + cl == NPAIR - 1)) for cl in range(ncnk)],
                      reads=["actT%d" % (c0 + cl) for cl in range(ncnk)] + ["wsl%d" % ws_], writes=[pr(k)])
                if c0 + ncnk == NPAIR:
                    for b in range(4):
                        k = PA[b]
                        T("dve", lambda e, k=k, b=b, h=h: e.scalar_tensor_tensor(
                            out=xst[:, b, h * 512:(h + 1) * 512], in0=xst[:, b, h * 512:(h + 1) * 512], scalar=ALPHA,
                            in1=psum[k][:], op0=ALU.mult, op1=ALU.add),
                          reads=[pr(k), "xs%d_%d" % (s, b)], writes=["xs%d_%d" % (s, b)])
                yield "dn"

        def drain(g):
            for _ in g:
                pass

        for g in (6, 4, 7, 5, 0, 1, 2, 3):
            cast_in(g)
        for i in range(4):
            cast_out(i)
        jc = 0
        NEARLY = 10
        for pi_, _ in enumerate(mixer_a(0)):
            if pi_ == 2:
                build_dg32()
                if ntiles > 1:
                    xload_bf(1)
            if jc < NEARLY:
                cast_up(jc)
                jc += 1
        while jc < NEARLY:
            cast_up(jc)
            jc += 1
        drain(mixer_b(0))
        for t in range(ntiles):
            ga = gb = None
            unit = 0
            more = t + 1 < ntiles
            if more:
                ga = mixer_a(t + 1)
                gb = mixer_b(t + 1)
            gf = ffn(t)
            gl = ln2_store(t - 1) if t > 0 else None
            while True:
                S.ctx = "F%d.%d" % (t, unit + 1)
                try:
                    next(gf)
                except StopIteration:
                    break
                unit += 1
                if t == 0:
                    if NEARLY + unit - 1 < NPAIR:
                        cast_up(NEARLY + unit - 1)
                    elif 0 <= unit - 13 < 12:
                        cast_dn(unit - 13)
                if gl is not None:
                    S.ctx = "L%d@%d" % (t - 1, unit)
                    try:
                        next(gl)
                    except StopIteration:
                        gl = None
                if unit == 10 and t + 2 < ntiles:
                    xload_bf(t + 2)
                if not more:
                    continue
                if unit == 12:
                    xload(t + 1)
                if unit > 0:
                    S.ctx = "M%d@%d" % (t + 1, unit)
                    if ga is not None:
                        try:
                            next(ga)
                        except StopIteration:
                            ga = None
                    elif gb is not None and unit > NPAIR + 2:
                        try:
                            next(gb)
                        except StopIteration:
                            gb = None
            S.ctx = "M%d@end" % (t + 1)
            if ga is not None:
                drain(ga)
            if gb is not None:
                drain(gb)
        S.ctx = "L%d@end" % (ntiles - 1)
        drain(ln2_store(ntiles - 1))
        global LAST_LABELS
        LAST_LABELS = S.labels
        S.emit(final_waits=[(("d", D_O[0]), S.dcnt[D_O[0]]), (("d", D_O[1]), S.dcnt[D_O[1]])])
        print("sbuf bytes remaining per partition:", nc.sbuf_bytes_remaining)
    return nc


_PARAM_NAMES = ["w_in", "b_in", "ln_a_g", "ln_a_b", "w_spatial", "b_spatial", "conv_b_w", "conv_b_b", "ln_b_g",
                "ln_b_b", "w_out", "b_out", "ln1_g", "ln1_b", "w_up", "conv_f_w", "conv_f_b", "w_down", "ln2_g", "ln2_b"]


def kernel(**inputs):
    x = np.ascontiguousarray(np.asarray(inputs["x"], dtype=np.float32))
    params = {k: np.ascontiguousarray(np.asarray(inputs[k], dtype=np.float32)) for k in _PARAM_NAMES}
    n = x.shape[0]
    nc = build_nc()
    in_maps = []
    for c in range(n):
        m = {"x": x[c]}
        m.update(params)
        in_maps.append(m)
    res = run_bass_kernel_spmd(nc, in_maps, core_ids=list(range(n)))
    return np.stack([np.asarray(r["out"], dtype=np.float32) for r in res.results], axis=0)
```
